# Optimizing a Trainium2 kernel written in Bass

```python
import jax, jax.numpy as jnp
from jax import lax
import numpy as np

D_MODEL = 1024
BATCH = 4
SEQ = 8192
DEPTH = 2

GM_WIDTH = D_MODEL
GM_GROUPS = 8
GM_GROUP_DIM = GM_WIDTH // GM_GROUPS
GM_CHUNK = 128
SSM_D_INNER = 2 * D_MODEL
SSM_HEAD_DIM = 64
SSM_HEADS = SSM_D_INNER // SSM_HEAD_DIM
SSM_GROUPS = 8
SSM_HEADS_PER_GROUP = SSM_HEADS // SSM_GROUPS
SSM_STATE = 128
SSM_CONV = 4
SSM_CHUNK = 128
SSM_CONV_DIM = SSM_D_INNER + 2 * SSM_GROUPS * SSM_STATE
N_BRANCH = 2
SPLITS = (
    GM_WIDTH,
    2 * GM_WIDTH,
    3 * GM_WIDTH,
    3 * GM_WIDTH + SSM_D_INNER,
    3 * GM_WIDTH + SSM_D_INNER + SSM_CONV_DIM,
    3 * GM_WIDTH + SSM_D_INNER + SSM_CONV_DIM + SSM_HEADS,
    3 * GM_WIDTH + SSM_D_INNER + SSM_CONV_DIM + SSM_HEADS + D_MODEL,
)
N_IN = 3 * GM_WIDTH + SSM_D_INNER + SSM_CONV_DIM + SSM_HEADS + N_BRANCH * D_MODEL
EPS = 1e-6

kernel_name = "hybrid_gmlp_ssd_gated_merge"


def rms_norm(x, w):
    xf = x.astype(jnp.float32)
    y = xf * lax.rsqrt(jnp.mean(xf * xf, axis=-1, keepdims=True) + EPS)
    return (y * w.astype(jnp.float32)).astype(x.dtype)


def layer_norm(x, w, b):
    xf = x.astype(jnp.float32)
    mu = jnp.mean(xf, axis=-1, keepdims=True)
    var = jnp.mean(jnp.square(xf - mu), axis=-1, keepdims=True)
    y = (xf - mu) * lax.rsqrt(var + EPS)
    return (y * w.astype(jnp.float32) + b.astype(jnp.float32)).astype(x.dtype)


def spatial_gating(u, v, ln_w, ln_b, w_s, b_s):
    b, s, _ = v.shape
    v = layer_norm(v, ln_w, ln_b)
    v = v.reshape(b, s // GM_CHUNK, GM_CHUNK, GM_GROUPS, GM_GROUP_DIM)
    mask = jnp.tril(jnp.ones((GM_CHUNK, GM_CHUNK), dtype=bool))
    w = jnp.where(mask[None], w_s, jnp.zeros_like(w_s))
    mixed = jnp.einsum('gts,bnsgd->bntgd', w, v) + b_s.T[None, None, :, :, None]
    return u * mixed.reshape(b, s, GM_WIDTH)


def causal_depthwise_conv(x, w, bias):
    k, ch = w.shape
    out = lax.conv_general_dilated(
        x, w[:, None, :], window_strides=(1,), padding=[(k - 1, 0)],
        dimension_numbers=('NWC', 'WIO', 'NWC'), feature_group_count=ch)
    return out + bias


def segsum_exp(a):
    t = a.shape[-1]
    cs = jnp.cumsum(a, axis=-1)
    diff = cs[..., :, None] - cs[..., None, :]
    mask = jnp.tril(jnp.ones((t, t), dtype=bool))
    return jnp.exp(jnp.where(mask, diff, -jnp.inf))


def ssd_scan(x, dt, a, bmat, cmat):
    b, s, h, p = x.shape
    q = SSM_CHUNK
    nc = s // q
    g, r, n = SSM_GROUPS, SSM_HEADS_PER_GROUP, SSM_STATE
    xd = (x * dt[..., None]).reshape(b, nc, q, g, r, p)
    adt = jnp.moveaxis((dt * a).astype(jnp.float32).reshape(b, nc, q, g, r), 2, -1)
    a_cs = jnp.cumsum(adt, axis=-1)
    bc = bmat.reshape(b, nc, q, g, n)
    cc = cmat.reshape(b, nc, q, g, n)
    decay = segsum_exp(adt)
    cb = jnp.einsum('bclgn,bcsgn->bcgls', cc, bc)
    y_diag = jnp.einsum('bcgls,bcgrls,bcsgrp->bclgrp', cb, decay, xd)
    decay_to_end = jnp.exp(a_cs[..., -1:] - a_cs)
    chunk_states = jnp.einsum('bcsgn,bcgrs,bcsgrp->bcgrpn', bc, decay_to_end, xd).astype(jnp.float32)
    chunk_decay = jnp.exp(a_cs[..., -1])

    def step(state, inp):
        cs, cd = inp
        return state * cd[..., None, None] + cs, state

    init = jnp.zeros((b, g, r, p, n), jnp.float32)
    _, prev_states = lax.scan(step, init, (jnp.moveaxis(chunk_states, 1, 0), jnp.moveaxis(chunk_decay, 1, 0)))
    prev_states = jnp.moveaxis(prev_states, 0, 1)
    y_off = jnp.einsum('bclgn,bcgrpn,bcgrl->bclgrp', cc, prev_states, jnp.exp(a_cs))
    return (y_diag + y_off).reshape(b, s, h, p).astype(x.dtype)


def hybrid_layer(x, c, ada_w, ada_b, norm_w, w_in, gm_ln_w, gm_ln_b, gm_ws, gm_bs,
                 conv_w, conv_b, dt_bias, a_log, d_skip, ssm_norm_w, w_proj_a, w_proj_b, w_out):
    b, s, _ = x.shape
    mod = jax.nn.silu(c) @ ada_w + ada_b
    shift, scale, gate = jnp.split(mod, 3, axis=-1)
    h = rms_norm(x, norm_w) * (1 + scale[:, None, :]) + shift[:, None, :]
    proj = h @ w_in
    gm_u, gm_v, gm_z, ssm_z, xbc, dt_raw, g_a, g_b = jnp.split(proj, SPLITS, axis=-1)
    y_a = spatial_gating(jax.nn.gelu(gm_u), jax.nn.gelu(gm_v), gm_ln_w, gm_ln_b, gm_ws, gm_bs) * jax.nn.silu(gm_z)
    xbc = jax.nn.silu(causal_depthwise_conv(xbc, conv_w, conv_b))
    xs, bm, cm = jnp.split(xbc, (SSM_D_INNER, SSM_D_INNER + SSM_GROUPS * SSM_STATE), axis=-1)
    dt = jax.nn.softplus((dt_raw + dt_bias).astype(jnp.float32))
    a = -jnp.exp(a_log.astype(jnp.float32))
    xh = xs.reshape(b, s, SSM_HEADS, SSM_HEAD_DIM)
    y_b = ssd_scan(xh, dt, a, bm.reshape(b, s, SSM_GROUPS, SSM_STATE), cm.reshape(b, s, SSM_GROUPS, SSM_STATE))
    y_b = y_b + xh * d_skip[:, None]
    yz = (y_b.reshape(b, s, SSM_D_INNER) * jax.nn.silu(ssm_z)).reshape(b, s, SSM_GROUPS, SSM_D_INNER // SSM_GROUPS)
    y_b = rms_norm(yz, ssm_norm_w.reshape(SSM_GROUPS, -1)).reshape(b, s, SSM_D_INNER)
    merged = jax.nn.sigmoid(g_a) * (y_a @ w_proj_a) + jax.nn.sigmoid(g_b) * (y_b @ w_proj_b)
    return x + gate[:, None, :] * (merged @ w_out)


def setup_inputs(seed: int = 0) -> dict:
    key = jax.random.key(seed)
    ks = jax.random.split(key, 24)
    nrm = jax.random.normal
    L, D = DEPTH, D_MODEL
    dt0 = jnp.exp(jax.random.uniform(ks[10], (L, SSM_HEADS), minval=np.log(1e-3), maxval=np.log(1e-1)))
    return {
        "x": nrm(ks[0], (BATCH, SEQ, D), jnp.float32),
        "c": nrm(ks[1], (BATCH, D), jnp.float32),
        "ada_w": nrm(ks[2], (L, D, 3 * D), jnp.float32) * D ** -0.5,
        "ada_b": 0.01 * nrm(ks[3], (L, 3 * D), jnp.float32),
        "norm_w": 1.0 + 0.1 * nrm(ks[4], (L, D), jnp.float32),
        "w_in": nrm(ks[5], (L, D, N_IN), jnp.float32) * D ** -0.5,
        "gm_ln_w": 1.0 + 0.1 * nrm(ks[6], (L, GM_WIDTH), jnp.float32),
        "gm_ln_b": 0.01 * nrm(ks[7], (L, GM_WIDTH), jnp.float32),
        "gm_ws": nrm(ks[8], (L, GM_GROUPS, GM_CHUNK, GM_CHUNK), jnp.float32) * GM_CHUNK ** -0.5,
        "gm_bs": 1.0 + 0.1 * nrm(ks[9], (L, GM_GROUPS, GM_CHUNK), jnp.float32),
        "conv_w": nrm(ks[11], (L, SSM_CONV, SSM_CONV_DIM), jnp.float32) * SSM_CONV ** -0.5,
        "conv_b": 0.01 * nrm(ks[12], (L, SSM_CONV_DIM), jnp.float32),
        "dt_bias": dt0 + jnp.log(-jnp.expm1(-dt0)),
        "a_log": jnp.log(jax.random.uniform(ks[13], (L, SSM_HEADS), minval=1.0, maxval=16.0)),
        "d_skip": 1.0 + 0.1 * nrm(ks[14], (L, SSM_HEADS), jnp.float32),
        "ssm_norm_w": 1.0 + 0.1 * nrm(ks[15], (L, SSM_D_INNER), jnp.float32),
        "w_proj_a": nrm(ks[16], (L, GM_WIDTH, D), jnp.float32) * GM_WIDTH ** -0.5,
        "w_proj_b": nrm(ks[17], (L, SSM_D_INNER, D), jnp.float32) * SSM_D_INNER ** -0.5,
        "w_out": nrm(ks[18], (L, D, D), jnp.float32) * D ** -0.5,
        "final_norm_w": 1.0 + 0.1 * nrm(ks[19], (D,), jnp.float32),
    }


def reference(x, c, ada_w, ada_b, norm_w, w_in, gm_ln_w, gm_ln_b, gm_ws, gm_bs, conv_w, conv_b,
              dt_bias, a_log, d_skip, ssm_norm_w, w_proj_a, w_proj_b, w_out, final_norm_w):
    for i in range(DEPTH):
        x = hybrid_layer(x, c, ada_w[i], ada_b[i], norm_w[i], w_in[i], gm_ln_w[i], gm_ln_b[i],
                         gm_ws[i], gm_bs[i], conv_w[i], conv_b[i], dt_bias[i], a_log[i], d_skip[i],
                         ssm_norm_w[i], w_proj_a[i], w_proj_b[i], w_out[i])
    return rms_norm(x, final_norm_w)
```

```python
import types
import numpy as np
from contextlib import ExitStack
import concourse.bass as bass
import concourse.mybir as mybir
from concourse.bass_utils import run_bass_kernel_spmd

F32 = mybir.dt.float32
BF16 = mybir.dt.bfloat16
AF = mybir.ActivationFunctionType
ALU = mybir.AluOpType

ENG_NAMES = ("pe", "act", "dve", "pool", "sp")
EPS = 1e-6


class Op:
    __slots__ = ("eng", "fn", "deps", "signal", "semval", "is_dma", "semkey", "dmacount")

    def __init__(self, eng, fn, is_dma=False, semkey=None):
        self.eng = eng
        self.fn = fn
        self.deps = []
        self.signal = False
        self.semval = 0
        self.is_dma = is_dma
        self.semkey = semkey
        self.dmacount = 0


def _freeze(fn):
    if fn.__closure__ is None:
        return fn
    cells = []
    for c in fn.__closure__:
        try:
            cells.append(types.CellType(c.cell_contents))
        except ValueError:
            cells.append(c)
    g = types.FunctionType(fn.__code__, fn.__globals__, fn.__name__, fn.__defaults__, tuple(cells))
    g.__kwdefaults__ = fn.__kwdefaults__
    return g


class Prog:
    def __init__(self, same_engine_sync=True):
        self.ops = {e: [] for e in ENG_NAMES}
        self.lastw = {}
        self.readers = {}
        self.dma_counts = {}
        self.same_engine_sync = same_engine_sync
        self.order = []
        self.marks = []
        self.kept = None

    def mark(self, name):
        self.marks.append((name, len(self.order)))

    def truncate(self, n):
        keep = set(id(o) for o in self.order[:n])
        for e in ENG_NAMES:
            self.ops[e] = [o for o in self.ops[e] if id(o) in keep]
        self.order = self.order[:n]
        self.kept = keep
        self.dma_counts = {}
        for o in self.order:
            if o.is_dma:
                self.dma_counts[o.semkey] = max(self.dma_counts.get(o.semkey, 0), o.dmacount)

    def _deps(self, op, reads, writes):
        psr = [k for k in reads if isinstance(k, tuple) and k[0] == "ps"]
        if psr:
            reads = [k for k in reads if k not in psr]
            writes = list(writes) + psr
        deps = []
        for k in list(reads) + list(writes):
            w = self.lastw.get(k)
            if w is not None:
                deps.append(w)
        for k in writes:
            deps.extend(self.readers.get(k, ()))
        seen = set()
        out = []
        for d in deps:
            if id(d) in seen or d is op:
                continue
            seen.add(id(d))
            out.append(d)
        if self.kept is not None:
            out = [d for d in out if id(d) in self.kept]
        op.deps = out
        for k in reads:
            self.readers.setdefault(k, []).append(op)
        for k in writes:
            self.lastw[k] = op
            self.readers[k] = []

    def add(self, eng, fn, reads=(), writes=()):
        op = Op(eng, _freeze(fn))
        self._deps(op, reads, writes)
        self.ops[eng].append(op)
        self.order.append(op)
        return op

    def dma(self, eng, fn, reads=(), writes=(), semkey=None):
        op = Op(eng, _freeze(fn), is_dma=True, semkey=semkey)
        self._deps(op, reads, writes)
        c = self.dma_counts.get(semkey, 0) + 16
        self.dma_counts[semkey] = c
        op.dmacount = c
        self.ops[eng].append(op)
        self.order.append(op)
        return op

    def _skip(self, op, d):
        return (not d.is_dma) and (not op.is_dma) and d.eng == op.eng and (op.eng == "pe" or not self.same_engine_sync)

    def emit(self, nc):
        for e in ENG_NAMES:
            for op in self.ops[e]:
                for d in op.deps:
                    if d.is_dma or self._skip(op, d):
                        continue
                    d.signal = True
        for e in ENG_NAMES:
            c = 0
            for op in self.ops[e]:
                if op.signal and not op.is_dma:
                    c += 1
                    op.semval = c
        engs = {"pe": nc.tensor, "act": nc.scalar, "dve": nc.vector, "pool": nc.gpsimd, "sp": nc.sync}
        with ExitStack() as st:
            esem = {e: st.enter_context(nc.semaphore("s_" + e)) for e in ENG_NAMES}
            dsem = {k: st.enter_context(nc.semaphore("d_%d" % i)) for i, k in enumerate(self.dma_counts)}
            block = st.enter_context(nc.Block())

            def run(ename):
                eng = engs[ename]
                waited = {}
                for op in self.ops[ename]:
                    for d in op.deps:
                        if d.is_dma:
                            s, v = dsem[d.semkey], d.dmacount
                        else:
                            if self._skip(op, d):
                                continue
                            s, v = esem[d.eng], d.semval
                        if waited.get(id(s), 0) >= v:
                            continue
                        waited[id(s)] = v
                        eng.wait_ge(s, v)
                    ins = op.fn(eng)
                    if op.is_dma:
                        ins.then_inc(dsem[op.semkey], 16)
                    elif op.signal:
                        ins.then_inc(esem[ename], 1)
                fin = {}
                for op in self.ops[ename]:
                    if op.is_dma:
                        fin[op.semkey] = max(fin.get(op.semkey, 0), op.dmacount)
                for k, v in fin.items():
                    if waited.get(id(dsem[k]), 0) < v:
                        eng.wait_ge(dsem[k], v)

            @block.tensor
            def _(e):
                run("pe")

            @block.scalar
            def _(e):
                run("act")

            @block.vector
            def _(e):
                run("dve")

            @block.gpsimd
            def _(e):
                run("pool")

            @block.sync
            def _(e):
                run("sp")


C_U, C_V, C_Z, C_SZ, C_XBC, C_DT, C_GA, C_GB = 0, 1024, 2048, 3072, 5120, 9216, 9248, 10272
N_IN = 11296
NSLOT = 3
SAME_ENGINE_SYNC = True

LAYER_INPUTS = [
    ("ada_w", [1024, 3072]), ("adab_col", [128, 24]), ("adab_gate_bc", [128, 1024]), ("normw_col", [128, 8]),
    ("w_in", [1024, N_IN]), ("lnw_bc", [128, 1024]), ("lnb_bc", [128, 1024]), ("gm_ws", [8, 128, 128]),
    ("bs_row", [1, 1024]), ("convw_col", [128, 4, 32]), ("convb_col", [128, 32]), ("convb_row", [1, 4096]),
    ("dtbias_bc", [128, 32]), ("alog_bc", [128, 32]), ("dskip_bc", [128, 32]), ("ssmnw_col", [128, 16]),
    ("w_proj_a", [1024, 1024]), ("w_proj_b", [2048, 1024]), ("w_out", [1024, 1024]),
]


def build(S, NL, NCH, do_final, debug=None, dump=False, layer_major=True):
    T = NCH * 128
    NT = S // T
    assert S % T == 0
    nc = bass.Bass("TRN2", target_bir_lowering=False)
    P = Prog(same_engine_sync=SAME_ENGINE_SYNC)

    def din(name, shape):
        return nc.dram_tensor(name, shape, F32, kind="ExternalInput").ap()

    x_d = din("x", [S, 1024])
    c_d = din("c_col", [128, 8])
    fnw_d = din("fnw_bc", [128, 1024])
    out_d = nc.dram_tensor("out", [S, 1024], F32, kind="ExternalOutput").ap()
    LD = []
    for l in range(NL):
        d = {n: din("%s_%d" % (n, l), sh) for n, sh in LAYER_INPUTS}
        d["win_bf"] = nc.dram_tensor("win_bf_%d" % l, [1024, N_IN], BF16, kind="Internal").ap()
        d["wa_bf"] = nc.dram_tensor("wa_bf_%d" % l, [1024, 1024], BF16, kind="Internal").ap()
        d["wb_bf"] = nc.dram_tensor("wb_bf_%d" % l, [2048, 1024], BF16, kind="Internal").ap()
        d["wo_bf"] = nc.dram_tensor("wo_bf_%d" % l, [1024, 1024], BF16, kind="Internal").ap()
        d["diag_bf"] = nc.dram_tensor("diag_bf_%d" % l, [4, 128, 4096], BF16, kind="Internal").ap()
        LD.append(d)

    st = ExitStack()
    with st:
        def sb(name, shape, dt):
            return st.enter_context(nc.sbuf_tensor(name, shape, dt))

        xt = sb("xt", [128, NCH, 1024], F32)
        xn = sb("xn", [128, 1024], BF16)
        hT = sb("hT", [128, 8, T], BF16)
        R1 = sb("R1", [128, 32 * T], BF16)
        R2 = sb("R2", [128, 32 * (T + 3)], BF16)
        yaT = sb("yaT", [128, 8, T], BF16)
        CT = sb("CT", [128, 8, T], BF16)
        wsl = [sb("wsl%d" % i, [128, 4096], BF16) for i in range(NSLOT)]
        gu = R1[:, 0:8 * T].rearrange("p (j t) -> p j t", j=8)
        sz = R1[:, 8 * T:16 * T].rearrange("p (j t) -> p j t", j=8)
        gv = R1[:, 16 * T:24 * T].rearrange("p (c n) -> p c n", c=NCH)
        vn = R1[:, 24 * T:32 * T].rearrange("p (c n) -> p c n", c=NCH)
        xs_tok = R1[:, 0:16 * T].rearrange("p (c n) -> p c n", c=NCH)
        B_tok = R1[:, 16 * T:24 * T].rearrange("p (c n) -> p c n", c=NCH)
        BT = R1[:, 24 * T:32 * T].rearrange("p (j t) -> p j t", j=8)
        sga = gu
        sgb = sz
        xbcT = R2[:, :].rearrange("p (j t) -> p j t", j=32)
        szs = R2[:, 0:16 * T].rearrange("p (c n) -> p c n", c=NCH)
        ynT = R2[:, 16 * (T + 3):16 * (T + 3) + 16 * T].rearrange("p (j t) -> p j t", j=16)
        mT = hT
        NB2 = 2
        rhsD = [sb("rhsD%d" % i, [128, 4, 128], F32) for i in range(NB2)]
        expD = [sb("expD%d" % i, [128, 4, 128], BF16) for i in range(NB2)]
        Gt = [sb("G%d" % i, [128, 4, 128], BF16) for i in range(4)]
        CBm = sb("CBm", [128, 8, 128], BF16)
        xd = sb("xd", [128, 32, 64], BF16)
        xdte = sb("xdte", [128, 32, 64], BF16)
        xsD = sb("xsD", [128, 32, 64], BF16)
        yz = sb("yz", [128, 8, 256], F32)
        tmpy = [sb("tmpy0", [128, 8, 64], F32)] * 2
        yn = sb("yn", [128, 8, 256], BF16)
        junk = yn[:, :, :].rearrange("p g n -> p (g n)")[:, 0:1024]
        dtb = sb("dtb", [128, NCH, 32], F32)
        dtt = sb("dtt", [128, NCH, 32], F32)
        adt = sb("adt", [128, NCH, 32], F32)
        acs_a = sb("acs_a", [128, NCH, 32], F32)
        el_a = sb("el_a", [128, NCH, 32], F32)
        dec_a = sb("dec_a", [128, NCH, 32], F32)
        dd_a = sb("dd_a", [128, NCH, 32], F32)
        dtdte_a = sb("dtdte_a", [128, NCH, 32], F32)
        ssq = sb("ssq", [128, 8], F32)
        rstd8 = sb("rstd8", [128, 8], F32)
        st1 = sb("st1", [128, 8], F32)
        stp = sb("stp", [128, NCH, 4], F32)
        bnst = sb("bnst", [128, 2, 6], F32)
        bnag = sb("bnag", [128, 2], F32)
        ident = sb("ident", [128, 128], BF16)
        identf = sb("identf", [128, 128], F32)
        triu = sb("triu", [128, 128], F32)
        mgt = sb("mgt", [128, 128], F32)
        onesf = sb("onesf", [128, 128], F32)
        ones_row = sb("ones_row", [128, 128], BF16)
        sel3 = sb("sel3", [128, 3, 128], BF16)
        fnw = sb("fnw", [128, 1024], F32) if do_final else None
        ccol = sb("ccol", [128, 8], F32)
        sc2 = sb("sc2", [128, 8, 2], F32)
        R2f = R2[:, :].bitcast(F32)
        screp = R2f[:, 0:1024].rearrange("p (k m) -> p k m", k=8)
        modc = sb("modc", [128, 32], F32)
        neghalf = sb("neghalf", [128, 8], F32)
        LS = []
        for l in range(1 if layer_major else NL):
            s = {}
            s["S"] = sb("S_%d" % l, [128, 8, 256], F32)
            s["Sb"] = sb("Sb_%d" % l, [128, 8, 256], BF16)
            s["lnw"] = sb("lnw_%d" % l, [128, 1024], BF16)
            s["lnb"] = sb("lnb_%d" % l, [128, 1024], BF16)
            s["WsT"] = sb("WsT_%d" % l, [128, 8, 128], BF16)
            s["brow"] = sb("brow_%d" % l, [128, 1024], BF16)
            s["crow"] = sb("crow_%d" % l, [128, 1024], BF16)
            s["convb"] = sb("convb_%d" % l, [128, 32], F32)
            s["dtbias"] = sb("dtbias_%d" % l, [128, 32], F32)
            s["a_bc"] = sb("abc_%d" % l, [128, 32], F32)
            s["dskip"] = sb("dskip_%d" % l, [128, 32], F32)
            s["gate"] = sb("gate_%d" % l, [128, 1024], BF16)
            s["weff"] = sb("weff_%d" % l, [128, 8], F32)
            s["shift"] = sb("shift_%d" % l, [128, 8], F32)
            s["halo"] = sb("halo_%d" % l, [128, 32, 3], BF16)
            s["wdt"] = sb("wdt_%d" % l, [128, 8, 32], BF16)
            LS.append(s)
        if layer_major:
            LS = LS * NL
        x1_d = nc.dram_tensor("x1_scr", [S, 1024], F32, kind="Internal").ap() if (layer_major and NL > 1) else None
        stage = {"adabc": sb("adabc", [128, 24], F32), "nwc": sb("nwc", [128, 8], F32), "cwc": sb("cwc", [128, 4, 32], F32), "alog": sb("alog", [128, 32], F32), "ssw": sb("ssw", [128, 16], F32)}
        ps = [st.enter_context(nc.psum_tensor("ps%d" % i, [128, 512], F32)) for i in range(8)]
        psb = [p[:, :].bitcast(BF16) for p in ps]

        dbg_slots = {}
        dbg_d = nc.dram_tensor("dbg2", [8, 128, 512], F32, kind="ExternalOutput").ap() if dump else None

        def DBG(tag, ap, keys):
            if not dump or tag not in dump or len(dbg_slots) >= 8:
                return
            i = len(dbg_slots)
            dbg_slots[tag] = i
            P.dma("pool", lambda e: e.dma_start(out=dbg_d[i], in_=ap), keys, [], semkey=("dbg2", i))

        bank_ctr = [0]

        def bank():
            i = bank_ctr[0] % 8
            bank_ctr[0] += 1
            return i

        def PK(i):
            return ("ps", i)

        act = lambda fn, r, w: P.add("act", fn, r, w)
        dve = lambda fn, r, w: P.add("dve", fn, r, w)
        pool = lambda fn, r, w: P.add("pool", fn, r, w)
        pe = lambda fn, r, w: P.add("pe", fn, r, w)

        pool(lambda e: e.memset(identf[:], 1.0), [], ["identf"])
        pool(lambda e: e.affine_select(out=identf[:], in_=identf[:], pattern=[[-1, 128]], compare_op=ALU.is_equal,
                                       fill=0.0, base=0, channel_multiplier=1), ["identf"], ["identf"])
        pool(lambda e: e.memset(onesf[:], 1.0), [], ["onesf"])
        pool(lambda e: e.affine_select(out=triu[:], in_=onesf[:], pattern=[[1, 128]], compare_op=ALU.is_ge,
                                       fill=0.0, base=0, channel_multiplier=-1), ["onesf"], ["triu"])
        pool(lambda e: e.affine_select(out=mgt[:], in_=onesf[:], pattern=[[-1, 128]], compare_op=ALU.is_gt,
                                       fill=0.0, base=0, channel_multiplier=1), ["onesf"], ["mgt"])
        dve(lambda e: e.tensor_copy(out=ident[:], in_=identf[:]), ["identf"], ["ident"])
        dve(lambda e: e.memset(ones_row[:], 0.0), [], ["ones_row"])
        dve(lambda e: e.memset(ones_row[0:2, :], 1.0), ["ones_row"], ["ones_row"])
        dve(lambda e: e.memset(neghalf[:], -0.5), [], ["neghalf"])
        dve(lambda e: e.memset(sel3[:], 1.0), [], ["sel3"])
        pool(lambda e: e.affine_select(out=sel3[:], in_=sel3[:], pattern=[[-1, 3], [0, 128]], compare_op=ALU.is_equal, fill=0.0, base=0,
                                       channel_multiplier=1), ["sel3"], ["sel3"])
        if do_final:
            P.dma("act", lambda e: e.dma_start(out=fnw[:], in_=fnw_d), [], ["fnw"], semkey="c_fnw")
        P.dma("act", lambda e: e.dma_start(out=ccol[:], in_=c_d), [], ["ccol"], semkey="c_ccol")
        act(lambda e: e.activation(out=sc2[:, :, 0], in_=ccol[:], func=AF.Silu), ["ccol"], ["sc2"])
        act(lambda e: e.activation(out=sc2[:, :, 1], in_=ccol[:], func=AF.Silu), ["ccol"], ["sc2"])

        P.mark("consts")
        W = 128 * NCH
        for l in range(NL):
            d = LD[l]
            L = lambda n, l=l: (n, l)
            for q in range(6):
                c0_, c1_ = q * 2048, min((q + 1) * 2048, N_IN)
                P.dma("pool", lambda e, c0_=c0_, c1_=c1_, d=d: e.dma_start(out=d["win_bf"][:, c0_:c1_], in_=d["w_in"][:, c0_:c1_],
                                                                            max_dma_last_dim=4096),
                      [], [("wscrc", l, q), ("castorder", l)], semkey=("castc", l, q))
            P.dma("pool", lambda e, d=d: e.dma_start(out=d["wa_bf"], in_=d["w_proj_a"], max_dma_last_dim=4096), [], [L("wscr"), ("castorder", l)], semkey=L("castw"))
            P.dma("pool", lambda e, d=d: e.dma_start(out=d["wo_bf"], in_=d["w_out"], max_dma_last_dim=4096), [], [L("wscr"), ("castorder", l)], semkey=L("castw"))
            P.mark("castdma%d" % l)

        def setup_layer(l):
            d, s = LD[l], LS[l]
            lk = 0 if layer_major else l
            L = lambda n: ("wscr", l) if n == "wscr" else (n, lk)
            ld = lambda dst, src, key: P.dma("act", lambda e: e.dma_start(out=dst, in_=src), [], [key], semkey=("ld", key))
            adabc, nwc, cwc, alog = stage["adabc"], stage["nwc"], stage["cwc"], stage["alog"]
            ld(adabc[:], d["adab_col"], L("adabc"))
            ld(nwc[:], d["normw_col"], L("nwc"))
            ld(cwc[:], d["convw_col"], L("cwc"))
            ld(alog[:], d["alog_bc"], L("alog"))
            ld(s["convb"][:], d["convb_col"], L("convb"))
            ld(s["dtbias"][:], d["dtbias_bc"], L("dtbias"))
            ld(s["dskip"][:], d["dskip_bc"], L("dskip"))
            P.dma("pool", lambda e, s=s, d=d: e.dma_start(out=s["lnw"][:], in_=d["lnw_bc"]), [], [L("lnw")], semkey=L("c_lnw"))
            P.dma("pool", lambda e, s=s, d=d: e.dma_start(out=s["lnb"][:], in_=d["lnb_bc"]), [], [L("lnb")], semkey=L("c_lnb"))
            P.mark("smallld%d" % l)
            ssw = stage["ssw"]
            ld(ssw[:], d["ssmnw_col"], L("ssw"))
            for kc in range(16):
                pf, pb = kc % 4, kc % 2
                stf = R2f[:, 4096 + pf * 1024:4096 + (pf + 1) * 1024]
                stb = R2[:, 6144 + pb * 1024:6144 + (pb + 1) * 1024]
                P.dma("act", lambda e, d=d, kc=kc, stf=stf: e.dma_start(out=stf, in_=d["w_proj_b"][kc * 128:(kc + 1) * 128, :]),
                      [], [("wbf", pf)] + (["R2a", "R2b"] if kc < 4 else []), semkey=L(("ld_wb", pf)))
                dve(lambda e, kc=kc, stf=stf, stb=stb: e.tensor_scalar(out=stb, in0=stf, scalar1=ssw[:, kc:kc + 1], scalar2=None, op0=ALU.mult),
                    [("wbf", pf), L("ssw")], [("wbb", pb)])
                P.dma("act", lambda e, d=d, kc=kc, stb=stb: e.dma_start(out=d["wb_bf"][kc * 128:(kc + 1) * 128, :], in_=stb),
                      [("wbb", pb)] + (["R2a", "R2b"] if kc >= 14 else []), [("wscr", l)], semkey=L(("st_wb", pb)))
            dve(lambda e: e.tensor_copy(out=screp, in_=sc2[:, :, 0:1].broadcast_to([128, 8, 128])), ["sc2"], ["screp", "R2a", "R2b"])
            P.dma("act", lambda e, d=d: e.dma_start(out=R2f[:, 1024:2048], in_=d["adab_gate_bc"]), [], ["R2a", "R2b"], semkey=L("ld_gate"))
            adaw_t = xt[:, :, :].rearrange("p c n -> p (c n)")
            adaw_v = adaw_t.rearrange("p (k n) -> p k n", k=8)
            bm = bank()
            for q in range(3072 // W):
                c0 = q * W
                P.dma("sp", lambda e, c0=c0, d=d: e.dma_start(out=adaw_v, in_=d["ada_w"].rearrange("(k p) n -> p k n", p=128)[:, :, c0:c0 + W]),
                      [], ["xt_all"] + [("xt", c_) for c_ in range(NCH)], semkey="adaw")
                if c0 < 2048:
                    for jj in range(NCH):
                        jidx = (c0 // 128) + jj
                        for k in range(8):
                            pe(lambda e, jidx=jidx, jj=jj, k=k, bm=bm: e.matmul(ps[bm][:, 2 * jidx:2 * jidx + 2],
                                                                                  lhsT=adaw_v[:, k, jj * 128:(jj + 1) * 128], rhs=sc2[:, k, :],
                                                                                  start=(k == 0), stop=(k == 7)),
                               ["xt_all", "sc2"], [PK(bm)])
                else:
                    bg = bank()
                    for k in range(8):
                        pe(lambda e, k=k, bg=bg: e.matmul(ps[bg][:, 0:W], lhsT=screp[:, k, :], rhs=adaw_v[:, k, :],
                                                          start=(k == 0), stop=(k == 7)),
                           ["xt_all", "screp", "R2a", "R2b"], [PK(bg)])
                    g0 = c0 - 2048
                    dve(lambda e, bg=bg, g0=g0, s=s: e.tensor_tensor(out=s["gate"][:, g0:g0 + W], in0=ps[bg][:, 0:W], in1=R2f[:, 1024 + g0:1024 + g0 + W], op=ALU.add),
                        [PK(bg), "R2a", "R2b"], [L("gate")])
            dve(lambda e, bm=bm: e.tensor_tensor(out=modc[:, 0:16], in0=ps[bm][:, 0:32].rearrange("p (j two) -> p j two", two=2)[:, :, 0],
                                                 in1=adabc[:, 0:16], op=ALU.add), [PK(bm), L("adabc")], ["modc"])
            dve(lambda e, s=s: e.tensor_copy(out=s["shift"][:], in_=modc[:, 0:8]), ["modc"], [L("shift")])
            dve(lambda e, s=s, nwc=nwc: e.scalar_tensor_tensor(out=s["weff"][:], in0=modc[:, 8:16], scalar=1.0, in1=nwc[:], op0=ALU.add, op1=ALU.mult),
                ["modc", L("nwc")], [L("weff")])
            P.mark("adaln%d" % l)
            wst_f = yz[:, :, :].rearrange("p g n -> p (g n)")[:, 0:1024].rearrange("p (g s) -> p g s", g=8)
            P.dma("act", lambda e, d=d: e.dma_start(out=wst_f, in_=d["gm_ws"].rearrange("g t s -> t g s")), [], ["yz"], semkey="ld_ws")
            pool(lambda e: e.affine_select(out=wst_f, in_=wst_f, pattern=[[0, 8], [-1, 128]], compare_op=ALU.is_ge, fill=0.0, base=0,
                                           channel_multiplier=1), ["yz"], ["yz"])
            dve(lambda e: e.tensor_copy(out=junk[:, :].rearrange("p (g s) -> p g s", g=8), in_=wst_f), ["yz"], ["yn"])
            bw = bank()
            for g in range(8):
                pe(lambda e, g=g, bw=bw: e.transpose(out=psb[bw][:, g * 128:(g + 1) * 128], in_=junk[:, g * 128:(g + 1) * 128], identity=ident[:]),
                   ["yn", "ident"], [PK(bw)])
            dve(lambda e, bw=bw, s=s: e.tensor_copy(out=s["WsT"][:].rearrange("p g t -> p (g t)"), in_=psb[bw][:, 0:1024]), [PK(bw)], [L("WsT")])
            P.mark("wst%d" % l)
            P.dma("act", lambda e, d=d: e.dma_start(out=R2f[0:3, 0:1024], in_=d["convb_row"][:, 0:3072].rearrange("o (i n) -> (o i) n", i=3)),
                  ["screp"], ["R2a", "R2b", L("cbr")], semkey=L("ld_cbr"))
            P.dma("act", lambda e, d=d: e.dma_start(out=R2f[0:1, 1024:2048], in_=d["bs_row"]), ["screp"], ["R2a", "R2b", L("bsr")], semkey=L("ld_bsr"))
            pool(lambda e, s=s: e.memset(s["crow"][:, :], 0.0), [], [L("cbrhi")])
            pool(lambda e, s=s: e.memset(s["brow"][:, :], 0.0), [], [L("bsrhi"), L("bsrlo")])
            dve(lambda e, s=s: e.tensor_copy(out=s["crow"][0:3, :], in_=R2f[0:3, 0:1024]), [L("cbr"), "R2a", "R2b"], [L("cbrhi")])
            hi_b = R2[0:1, 4096:5120]
            lo_b = R2[0:1, 5120:6144]
            dve(lambda e, hi_b=hi_b: e.tensor_copy(out=hi_b, in_=R2f[0:1, 1024:2048]), [L("bsr"), "R2a", "R2b"], ["R2a", "R2b", L("hib")])
            dve(lambda e, hi_b=hi_b, lo_b=lo_b: e.tensor_tensor(out=lo_b, in0=R2f[0:1, 1024:2048], in1=hi_b, op=ALU.subtract), [L("bsr"), L("hib"), "R2a", "R2b"], ["R2a", "R2b", L("lob")])
            P.dma("act", lambda e, s=s, hi_b=hi_b: e.dma_start(out=s["brow"][0:1, :], in_=hi_b), [L("hib"), "R2a", "R2b"], [L("bsrhi")], semkey=L("mv_hi"))
            P.dma("act", lambda e, s=s, lo_b=lo_b: e.dma_start(out=s["brow"][1:2, :], in_=lo_b), [L("lob"), "R2a", "R2b"], [L("bsrlo")], semkey=L("mv_lo"))
            act(lambda e, s=s, alog=alog: e.activation(out=s["a_bc"][:], in_=alog[:], func=AF.Exp), [L("alog")], [L("a_bc")])
            dve(lambda e, s=s: e.tensor_scalar(out=s["a_bc"][:], in0=s["a_bc"][:], scalar1=-1.0, scalar2=None, op0=ALU.mult), [L("a_bc")], [L("a_bc")])
            P.mark("rows%d" % l)
            for q in range(4):
                stg = wsl[q % NSLOT][:, :].rearrange("p (j k n) -> p j k n", j=8, k=4)
                cw_q = cwc[:, :, 8 * q:8 * q + 8].rearrange("p k j -> p j k").unsqueeze(3).broadcast_to([128, 8, 4, 128])
                id_q = identf[:, :].unsqueeze(1).unsqueeze(1).broadcast_to([128, 8, 4, 128])
                dve(lambda e, stg=stg, cw_q=cw_q, id_q=id_q: e.tensor_tensor(out=stg, in0=id_q, in1=cw_q, op=ALU.mult),
                    ["identf", L("cwc")], [("w", q % NSLOT)])
                P.dma("sp", lambda e, q=q, d=d: e.dma_start(out=d["diag_bf"][q], in_=wsl[q % NSLOT][:, :]), [("w", q % NSLOT)], [L("wscr")], semkey=L("diagst"))
            P.dma("sp", lambda e, s=s, d=d: e.dma_start(out=s["wdt"][:], in_=d["win_bf"].rearrange("(k p) n -> p k n", p=128)[:, :, C_DT:C_DT + 32]),
                  win_parts(l, C_DT, 32), [L("wdt")], semkey=L("ld_wdt"))
            pool(lambda e, s=s: e.memset(s["S"][:], 0.0), [], [L(("S", gp)) for gp in range(4)])
            pool(lambda e, s=s: e.memset(s["Sb"][:], 0.0), [], [L(("Sb", gp)) for gp in range(4)])
            pool(lambda e, s=s: e.memset(s["halo"][:], 0.0), [], [L("halo")])

        if not layer_major:
            for l in range(NL):
                setup_layer(l)
        P.mark("setup_done")
        wctr = [0]

        def win_parts(l, c0, width=512):
            return [("wscrc", l, q) for q in range(c0 // 2048, (c0 + width - 1) // 2048 + 1)]

        def wload(l, src_ap_fn, first=False, extra=()):
            sl = wctr[0] % NSLOT
            wctr[0] += 1
            P.dma("sp", lambda e: e.dma_start(out=wsl[sl][:, :].rearrange("p (k n) -> p k n", k=8), in_=src_ap_fn()),
                  list(extra) if extra else [("wscr", l)], [("w", sl)], semkey=("wsl", sl))
            return sl

        def wload_raw(l, src_ap_fn):
            sl = wctr[0] % NSLOT
            wctr[0] += 1
            P.dma("sp", lambda e: e.dma_start(out=wsl[sl][:, :], in_=src_ap_fn()), [("wscr", l)], [("w", sl)], semkey=("wsl", sl))
            return sl

        def wview(sl):
            return wsl[sl][:, :].rearrange("p (k n) -> p k n", k=8)

        def blk(l, name, c0, r0=0):
            return lambda: LD[l][name].rearrange("(k p) n -> p k n", p=128)[:, r0:r0 + 8, c0:c0 + 512]

        HT_ALL = [("hT", c) for c in range(NCH)]

        def XS(c):
            return "R1a" if c < max(NCH // 2, 1) else "R1b"

        def proj_fm(l, c0, evac):
            sl = wload(l, blk(l, "win_bf", c0), extra=win_parts(l, c0))
            wv = wview(sl)
            for jj in range(4):
                b = bank()
                for k in range(8):
                    pe(lambda e, b=b, k=k, jj=jj, wv=wv: e.matmul(ps[b][:, 0:T], lhsT=wv[:, k, jj * 128:(jj + 1) * 128], rhs=hT[:, k, :],
                                                                   start=(k == 0), stop=(k == 7)),
                       [("w", sl)] + HT_ALL, [PK(b)])
                evac(b, jj)

        def proj_tm(l, c0, evac):
            sl = wload(l, blk(l, "win_bf", c0), extra=win_parts(l, c0))
            wv = wview(sl)
            for c in range(NCH):
                b = bank()
                for k in range(8):
                    pe(lambda e, b=b, k=k, c=c, wv=wv: e.matmul(ps[b][:, :], lhsT=hT[:, k, c * 128:(c + 1) * 128], rhs=wv[:, k, :],
                                                                 start=(k == 0), stop=(k == 7)),
                       [("w", sl), ("hT", c)], [PK(b)])
                evac(b, c)

        if layer_major:
            schedule = [(it, [l], l == 0, l == NL - 1, [l] if it == 0 else []) for l in range(NL) for it in range(NT)]
        else:
            schedule = [(it, list(range(NL)), True, True, []) for it in range(NT)]
        def emit_xload(src_d, src_x, it, c, after_setup=False):
            r0 = it * T + c * 128
            P.dma("pool", lambda e: e.dma_start(out=xt[:, c, :], in_=src_d[r0:r0 + 128, :]),
                  [] if src_x else [("x1", it, c)], [("xt", c)] + (["xt_all"] if after_setup else []), semkey=("xin", c))

        def emit_out_chunk(si, it, c, dst_out):
            nonlocal preloaded
            if do_final and dst_out:
                act(lambda e: e.activation(out=junk[:, :], in_=xt[:, c, :], func=AF.Square, scale=1.0 / 32, accum_out=st1[:, 0:1]), [("xt", c)], ["yn", "st1"])
                dve(lambda e: e.tensor_scalar(out=st1[:, 1:2], in0=st1[:, 0:1], scalar1=EPS, scalar2=None, op0=ALU.add), ["st1"], ["st1b"])
                pool(lambda e: e.tensor_tensor(out=st1[:, 2:3], in0=st1[:, 1:2], in1=neghalf[:, 0:1], op=ALU.pow), ["st1b", "neghalf"], ["st1c"])
                dve(lambda e: e.scalar_tensor_tensor(out=xt[:, c, :], in0=xt[:, c, :], scalar=st1[:, 2:3], in1=fnw[:, :], op0=ALU.mult, op1=ALU.mult),
                    ["st1c", ("xt", c), "fnw"], [("xt", c)])
            dst_d = out_d if dst_out else x1_d
            r0 = it * T + c * 128
            P.dma("act", lambda e: e.dma_start(out=dst_d[r0:r0 + 128, :], in_=xt[:, c, :]), [("xt", c)], [] if dst_out else [("x1", it, c)], semkey=("xout", c))
            if si + 1 < len(schedule) and not schedule[si + 1][4]:
                (it2, _, src_x2, _, _) = schedule[si + 1]
                emit_xload(x_d if src_x2 else x1_d, src_x2, it2, c)
                if c == NCH - 1:
                    preloaded = True

        preloaded = False
        for si, (it, layers, src_x, dst_out, setups) in enumerate(schedule):
            t0 = it * T
            for l_ in setups:
                setup_layer(l_)
            src_d = x_d if src_x else x1_d
            if not preloaded:
                for c in range(NCH):
                    emit_xload(src_d, src_x, it, c, after_setup=True)
            preloaded = False
            for l in layers:
                d, s = LD[l], LS[l]
                lk = 0 if layer_major else l
                L = lambda n, lk=lk: (n, lk)
                P.mark("t%d_l%d_start" % (it, l))
                for c in range(NCH):
                    act(lambda e, c=c: e.activation(out=junk[:, :], in_=xt[:, c, :], func=AF.Square, scale=1.0 / 32, accum_out=stp[:, c, 0:1]),
                        [("xt", c)], ["yn", ("stp", c)])
                for c in range(NCH):
                    dve(lambda e, c=c: e.tensor_scalar(out=stp[:, c, 1:2], in0=stp[:, c, 0:1], scalar1=EPS, scalar2=None, op0=ALU.add), [("stp", c)], [("stpb", c)])
                    pool(lambda e, c=c: e.tensor_tensor(out=stp[:, c, 2:3], in0=stp[:, c, 1:2], in1=neghalf[:, 0:1], op=ALU.pow), [("stpb", c), "neghalf"], [("stpc", c)])
                for c in range(NCH):
                    dve(lambda e, c=c: e.tensor_scalar(out=xn[:, :], in0=xt[:, c, :], scalar1=stp[:, c, 2:3], scalar2=None, op0=ALU.mult),
                        [("stpc", c), ("xt", c)], ["xn"])
                    b = bank()
                    for k in range(8):
                        pe(lambda e, b=b, k=k: e.transpose(out=psb[b][:, k * 128:(k + 1) * 128], in_=xn[:, k * 128:(k + 1) * 128], identity=ident[:]),
                           ["xn", "ident"], [PK(b)])
                    for k in range(8):
                        act(lambda e, b=b, k=k, c=c, s=s: e.activation(out=hT[:, k, c * 128:(c + 1) * 128], in_=psb[b][:, k * 128:(k + 1) * 128], func=AF.Identity,
                                                                        scale=s["weff"][:, k:k + 1], bias=s["shift"][:, k:k + 1]),
                            [PK(b), L("weff"), L("shift")], [("hT", c)])
                P.mark("t%d_l%d_A" % (it, l))
                for h2 in range(2):
                    proj_fm(l, C_U + h2 * 512, lambda b, jj, h2=h2: act(
                        lambda e: e.activation(out=gu[:, h2 * 4 + jj, :], in_=ps[b][:, 0:T], func=AF.Gelu_apprx_tanh), [PK(b)], ["R1a"]))
                for h2 in range(2):
                    proj_fm(l, C_Z + h2 * 512, lambda b, jj, h2=h2: act(
                        lambda e: e.activation(out=sz[:, h2 * 4 + jj, :], in_=ps[b][:, 0:T], func=AF.Silu), [PK(b)], ["R1b"]))
                pool(lambda e: e.tensor_tensor(out=gu[:, :, :], in0=gu[:, :, :], in1=sz[:, :, :], op=ALU.mult), ["R1a", "R1b"], ["R1a"])
                for h2 in range(2):
                    proj_tm(l, C_V + h2 * 512, lambda b, c, h2=h2: act(
                        lambda e: e.activation(out=gv[:, c, h2 * 512:(h2 + 1) * 512], in_=ps[b][:, :], func=AF.Gelu_apprx_tanh), [PK(b)], [("R1c", c)]))
                for c in range(NCH):
                    for h2 in range(2):
                        dve(lambda e, c=c, h2=h2: e.bn_stats(out=bnst[:, h2, :], in_=gv[:, c, h2 * 512:(h2 + 1) * 512]), [("R1c", c)], ["bnst"])
                    dve(lambda e: e.bn_aggr(out=bnag[:, :], in_=bnst[:, :, :].rearrange("p a b -> p (a b)")), ["bnst"], ["bnag"])
                    dve(lambda e: e.tensor_scalar(out=st1[:, 3:4], in0=bnag[:, 1:2], scalar1=EPS, scalar2=None, op0=ALU.add), ["bnag"], ["st1d"])
                    pool(lambda e: e.tensor_tensor(out=st1[:, 4:5], in0=st1[:, 3:4], in1=neghalf[:, 0:1], op=ALU.pow), ["st1d", "neghalf"], ["st1e"])
                    dve(lambda e, c=c, s=s: e.scalar_tensor_tensor(out=junk[:, :], in0=gv[:, c, :], scalar=bnag[:, 0:1], in1=s["lnw"][:, :], op0=ALU.subtract, op1=ALU.mult),
                        [("R1c", c), "bnag", L("lnw")], ["yn"])
                    dve(lambda e, c=c, s=s: e.scalar_tensor_tensor(out=vn[:, c, :], in0=junk[:, :], scalar=st1[:, 4:5], in1=s["lnb"][:, :], op0=ALU.mult, op1=ALU.add),
                        ["yn", "st1e", L("lnb")], ["R1d"])
                for g in range(8):
                    b = bank()
                    for c in range(NCH):
                        o = ps[b][:, c * 128:(c + 1) * 128]
                        pe(lambda e, o=o, c=c, g=g, s=s: e.matmul(o, lhsT=vn[:, c, g * 128:(g + 1) * 128], rhs=s["WsT"][:, g, :], start=True, stop=False),
                           ["R1d", L("WsT")], [PK(b)])
                        pe(lambda e, o=o, g=g, s=s: e.matmul(o, lhsT=ones_row[:, :], rhs=s["brow"][:, g * 128:(g + 1) * 128], start=False, stop=True),
                           ["ones_row", L("bsrhi"), L("bsrlo")], [PK(b)])
                    dve(lambda e, b=b, g=g: e.tensor_tensor(out=yaT[:, g, :], in0=ps[b][:, 0:T], in1=gu[:, g, :], op=ALU.mult), [PK(b), "R1a"], [("yaT", g)])
                P.mark("t%d_l%d_B" % (it, l))
                dve(lambda e, s=s: e.tensor_copy(out=xbcT[:, :, 0:3], in_=s["halo"][:, :, :]), [L("halo")], ["R2a", "R2b"])
                for q in range(8):
                    proj_fm(l, C_XBC + q * 512, lambda b, jj, q=q: act(
                        lambda e: e.activation(out=xbcT[:, q * 4 + jj, 3:3 + T], in_=ps[b][:, 0:T], func=AF.Copy), [PK(b)], ["R2a" if q < 4 else "R2b"]))
                dve(lambda e, s=s: e.tensor_copy(out=s["halo"][:, :, :], in_=xbcT[:, :, T:T + 3]), ["R2a", "R2b"], [L("halo")])
                P.mark("t%d_l%d_dt" % (it, l))
                bdt = bank()
                for c in range(NCH):
                    for k in range(8):
                        pe(lambda e, c=c, k=k, s=s, bdt=bdt: e.matmul(ps[bdt][:, c * 32:(c + 1) * 32], lhsT=hT[:, k, c * 128:(c + 1) * 128], rhs=s["wdt"][:, k, :],
                                                                       start=(k == 0), stop=(k == 7)),
                           [("hT", c), L("wdt")], [PK(bdt)])
                dve(lambda e, s=s, bdt=bdt: e.tensor_tensor(out=dtb[:, :, :], in0=ps[bdt][:, 0:NCH * 32].rearrange("p (c h) -> p c h", c=NCH),
                                                            in1=s["dtbias"][:, :].unsqueeze(1).broadcast_to([128, NCH, 32]), op=ALU.add),
                    [PK(bdt), L("dtbias")], ["dtb"])
                act(lambda e: e.activation(out=dtb[:, :, :], in_=dtb[:, :, :], func=AF.Exp), ["dtb"], ["dtb"])
                act(lambda e: e.activation(out=dtt[:, :, :], in_=dtb[:, :, :], func=AF.Ln, bias=1.0), ["dtb"], ["dtt"])
                dve(lambda e, s=s: e.tensor_tensor(out=adt[:, :, :], in0=dtt[:, :, :], in1=s["a_bc"][:, :].unsqueeze(1).broadcast_to([128, NCH, 32]), op=ALU.mult),
                    ["dtt", L("a_bc")], ["adt"])
                P.mark("t%d_l%d_conv" % (it, l))
                dsl = [wload_raw(l, lambda q=q, d=d: d["diag_bf"][q]) for q in range(2)]
                for c in range(NCH):
                    for qq in range(4):
                        b = bank()
                        for jj in range(4):
                            j = qq * 4 + jj
                            dg = wsl[dsl[j // 8]][:, :].rearrange("p (j k n) -> p j k n", j=8, k=4)
                            o = ps[b][:, jj * 128:(jj + 1) * 128]
                            for k in range(4):
                                pe(lambda e, o=o, j=j, k=k, c=c, dg=dg: e.matmul(o, lhsT=xbcT[:, j, c * 128 + k:c * 128 + k + 128], rhs=dg[:, j % 8, k, :],
                                                                                start=(k == 0), stop=False),
                                   ["R2a", ("w", dsl[j // 8])], [PK(b)])
                            pe(lambda e, o=o, j=j, s=s: e.matmul(o, lhsT=sel3[:, j // 8, :], rhs=s["crow"][:, (j % 8) * 128:(j % 8 + 1) * 128], start=False, stop=True),
                               ["sel3", L("cbrhi")], [PK(b)])
                        act(lambda e, b=b, c=c, qq=qq: e.activation(out=xs_tok[:, c, qq * 512:(qq + 1) * 512], in_=ps[b][:, :], func=AF.Silu), [PK(b)], [XS(c)])
                dsl2 = [wload_raw(l, lambda q=q, d=d: d["diag_bf"][q]) for q in (2, 3)]
                dgB = wsl[dsl2[0]][:, :].rearrange("p (j k n) -> p j k n", j=8, k=4)
                dgC = wsl[dsl2[1]][:, :].rearrange("p (j k n) -> p j k n", j=8, k=4)
                for g in range(8):
                    for (dg, j, dst, key) in ((dgB, 16 + g, BT, "R1dB"), (dgC, 24 + g, CT, "CT")):
                        b = bank()
                        for k in range(4):
                            pe(lambda e, b=b, j=j, g=g, k=k, dg=dg: e.matmul(ps[b][:, 0:T], lhsT=dg[:, g, k, :], rhs=xbcT[:, j, k:k + T], start=(k == 0), stop=(k == 3)),
                               ["R2b", ("w", dsl2[0]), ("w", dsl2[1])], [PK(b)])
                        wk = ["R1d"] if key == "R1dB" else [(key, g)]
                        act(lambda e, b=b, j=j, g=g, dst=dst, s=s: e.activation(out=dst[:, g, :], in_=ps[b][:, 0:T], func=AF.Silu, bias=s["convb"][:, j:j + 1]),
                            [PK(b), L("convb")], wk)
                for c in range(NCH):
                    b = bank()
                    for g in range(8):
                        pe(lambda e, b=b, g=g, c=c: e.transpose(out=psb[b][:, g * 128:(g + 1) * 128], in_=BT[:, g, c * 128:(c + 1) * 128], identity=ident[:]),
                           ["R1d", "ident"], [PK(b)])
                    act(lambda e, b=b, c=c: e.activation(out=B_tok[:, c, :], in_=psb[b][:, :], func=AF.Copy), [PK(b)], [("R1c", c)])
                P.mark("t%d_l%d_sz" % (it, l))
                for q in range(4):
                    proj_tm(l, C_SZ + q * 512, lambda b, c, q=q: act(
                        lambda e: e.activation(out=szs[:, c, q * 512:(q + 1) * 512], in_=ps[b][:, :], func=AF.Silu), [PK(b)], ["R2a"]))
                P.mark("t%d_l%d_ssd" % (it, l))
                bs_ = bank()
                for c in range(NCH):
                    pe(lambda e, c=c, bs_=bs_: e.matmul(ps[bs_][:, c * 64:c * 64 + 32], lhsT=triu[:, :], rhs=adt[:, c, :], start=True, stop=True), ["triu", "adt"], [PK(bs_)])
                    pe(lambda e, c=c, bs_=bs_: e.matmul(ps[bs_][:, c * 64 + 32:c * 64 + 64], lhsT=onesf[:, :], rhs=adt[:, c, :], start=True, stop=True), ["onesf", "adt"], [PK(bs_)])
                psv = ps[bs_][:, 0:NCH * 64].rearrange("p (c x) -> p c x", c=NCH)
                dve(lambda e, psv=psv: e.tensor_copy(out=acs_a[:, :, :], in_=psv[:, :, 0:32]), [PK(bs_)], ["acs"])
                act(lambda e, psv=psv: e.activation(out=el_a[:, :, :], in_=psv[:, :, 0:32], func=AF.Exp), [PK(bs_)], ["el"])
                act(lambda e, psv=psv: e.activation(out=dec_a[:, :, :], in_=psv[:, :, 32:64], func=AF.Exp), [PK(bs_)], ["dec"])
                dve(lambda e, psv=psv: e.tensor_tensor(out=dd_a[:, :, :], in0=psv[:, :, 32:64], in1=acs_a[:, :, :], op=ALU.subtract), [PK(bs_), "acs"], ["dd"])
                act(lambda e: e.activation(out=dd_a[:, :, :], in_=dd_a[:, :, :], func=AF.Exp), ["dd"], ["dd"])
                dve(lambda e: e.tensor_tensor(out=dtdte_a[:, :, :], in0=dtt[:, :, :], in1=dd_a[:, :, :], op=ALU.mult), ["dtt", "dd"], ["dtdte"])
                sctr = [0, 0]

                def bankD():
                    sctr[0] += 1
                    return sctr[0] % 2

                def bankS():
                    sctr[1] += 1
                    return 2 + sctr[1] % 6

                def emit_rhsD(c, g):
                    i3 = g % 2
                    pool(lambda e: e.tensor_tensor(out=rhsD[i3][:, :, :], in0=adt[:, c, 4 * g:4 * g + 4].unsqueeze(2).broadcast_to([128, 4, 128]),
                                                   in1=triu[:, :].unsqueeze(1).broadcast_to([128, 4, 128]), op=ALU.mult),
                         ["adt", "triu"], [("rhsD", i3)])
                    bD = bankD()
                    pe(lambda e: e.matmul(ps[bD][:, :], lhsT=mgt[:, :], rhs=rhsD[i3][:, :, :].rearrange("p r l -> p (r l)"), start=True, stop=True),
                       ["mgt", ("rhsD", i3)], [PK(bD)])
                    return bD

                def emit_G(c, g, bD):
                    i2 = g % 2
                    i4 = g % 4
                    act(lambda e: e.activation(out=expD[i2][:, :, :].rearrange("p r l -> p (r l)"), in_=ps[bD][:, :], func=AF.Exp), [PK(bD)], [("expD", i2)])
                    dve(lambda e: e.tensor_tensor(out=Gt[i4][:, :, :], in0=expD[i2][:, :, :], in1=CBm[:, g, :].unsqueeze(1).broadcast_to([128, 4, 128]), op=ALU.mult),
                        [("expD", i2), ("CBm", g // 4)], [("G", i4)])

                def head(c):
                    for half in range(2):
                        b = bankS()
                        for gg in range(4):
                            g = half * 4 + gg
                            pe(lambda e, b=b, gg=gg, g=g: e.matmul(ps[b][:, gg * 128:(gg + 1) * 128], lhsT=BT[:, g, c * 128:(c + 1) * 128], rhs=CT[:, g, c * 128:(c + 1) * 128],
                                                                   start=True, stop=True),
                               ["R1d", ("CT", g)], [PK(b)])
                        dve(lambda e, b=b, half=half: e.tensor_tensor(out=CBm[:, half * 4:(half + 1) * 4, :], in0=ps[b][:, :].rearrange("p (g l) -> p g l", g=4),
                                                                      in1=triu[:, :].unsqueeze(1).broadcast_to([128, 4, 128]), op=ALU.mult),
                            [PK(b), "triu"], [("CBm", half)])
                    return {0: emit_rhsD(c, 0), 1: emit_rhsD(c, 1)}

                pending_tail = []
                bDs_next = head(0)
                for c in range(NCH):
                    bDs = bDs_next
                    xsv = xs_tok[:, c, :].rearrange("p (h q) -> p h q", h=32)
                    pool(lambda e, c=c, xsv=xsv: e.tensor_tensor(out=xd[:, :, :], in0=xsv, in1=dtt[:, c, :].unsqueeze(2).broadcast_to([128, 32, 64]), op=ALU.mult),
                         [XS(c), "dtt"], ["xd"])
                    pool(lambda e, xsv=xsv, s=s: e.tensor_tensor(out=xsD[:, :, :], in0=xsv, in1=s["dskip"][:, :].unsqueeze(2).broadcast_to([128, 32, 64]), op=ALU.mult),
                         [XS(c), L("dskip")], ["xsD"])
                    pool(lambda e, c=c, xsv=xsv: e.tensor_tensor(out=xdte[:, :, :], in0=xsv, in1=dtdte_a[:, c, :].unsqueeze(2).broadcast_to([128, 32, 64]), op=ALU.mult),
                         [XS(c), "dtdte"], ["xdte"])
                    for f in pending_tail:
                        f()
                    pending_tail = []
                    emit_G(c, 0, bDs.pop(0))
                    emit_G(c, 1, bDs.pop(1))
                    for gp in range(4):
                        byd, byo, bst = bankS(), bankS(), bankS()
                        if gp + 1 < 4:
                            for g_ in (2 * gp + 2, 2 * gp + 3):
                                bDs[g_] = emit_rhsD(c, g_)
                        for gi in range(2):
                            g = gp * 2 + gi
                            i4 = g % 4
                            for r in range(4):
                                h = 4 * g + r
                                o = ps[byd][:, gi * 256 + r * 64:gi * 256 + (r + 1) * 64]
                                pe(lambda e, o=o, i4=i4, r=r, h=h: e.matmul(o, lhsT=Gt[i4][:, r, :], rhs=xd[:, h, :], start=True, stop=False), [("G", i4), "xd"], [PK(byd)])
                                pe(lambda e, o=o, h=h: e.matmul(o, lhsT=ident[:, :], rhs=xsD[:, h, :], start=False, stop=True), ["ident", "xsD"], [PK(byd)])
                            pe(lambda e, g=g, gi=gi, c=c, s=s, byo=byo: e.matmul(ps[byo][:, gi * 256:(gi + 1) * 256], lhsT=CT[:, g, c * 128:(c + 1) * 128], rhs=s["Sb"][:, g, :],
                                                                                 start=True, stop=True),
                               [("CT", g), L(("Sb", gp))], [PK(byo)])
                            pe(lambda e, g=g, gi=gi, c=c, bst=bst: e.matmul(ps[bst][:, gi * 256:(gi + 1) * 256], lhsT=B_tok[:, c, g * 128:(g + 1) * 128],
                                                                           rhs=xdte[:, 4 * g:4 * g + 4, :].rearrange("p r q -> p (r q)"), start=True, stop=True),
                               [("R1c", c), "xdte"], [PK(bst)])
                        if gp + 1 < 4:
                            for g_ in (2 * gp + 2, 2 * gp + 3):
                                emit_G(c, g_, bDs.pop(g_))
                        tb = tmpy[gp % 2]
                        h0 = gp * 8
                        dve(lambda e, tb=tb, h0=h0, byo=byo, c=c: e.tensor_tensor(out=tb[:, :, :], in0=ps[byo][:, :].rearrange("p (h q) -> p h q", h=8),
                                                                                   in1=el_a[:, c, h0:h0 + 8].unsqueeze(2).broadcast_to([128, 8, 64]), op=ALU.mult),
                            [PK(byo), "el"], [("tmpy", 0)])
                        dve(lambda e, tb=tb, byd=byd: e.tensor_tensor(out=tb[:, :, :], in0=ps[byd][:, :].rearrange("p (h q) -> p h q", h=8), in1=tb[:, :, :], op=ALU.add),
                            [PK(byd), ("tmpy", 0)], [("tmpy", 0)])
                        dve(lambda e, tb=tb, gp=gp, c=c: e.tensor_tensor(out=yz[:, 2 * gp:2 * gp + 2, :].rearrange("p g n -> p (g n)"), in0=tb[:, :, :].rearrange("p h q -> p (h q)"),
                                                                        in1=szs[:, c, gp * 512:(gp + 1) * 512], op=ALU.mult),
                            [("tmpy", 0), "R2a"], [("yz", gp)])
                        Sv = s["S"][:, 2 * gp:2 * gp + 2, :].rearrange("p g (r q) -> p (g r) q", r=4)
                        dve(lambda e, Sv=Sv, h0=h0, c=c: e.tensor_tensor(out=Sv, in0=Sv, in1=dec_a[:, c, h0:h0 + 8].unsqueeze(2).broadcast_to([128, 8, 64]), op=ALU.mult),
                            [L(("S", gp)), "dec"], [L(("S", gp))])
                        dve(lambda e, gp=gp, s=s, bst=bst: e.tensor_tensor(out=s["S"][:, 2 * gp:2 * gp + 2, :].rearrange("p g n -> p (g n)"), in0=ps[bst][:, :],
                                                                          in1=s["S"][:, 2 * gp:2 * gp + 2, :].rearrange("p g n -> p (g n)"), op=ALU.add),
                            [PK(bst), L(("S", gp))], [L(("S", gp))])
                        act(lambda e, gp=gp, s=s: e.activation(out=s["Sb"][:, 2 * gp:2 * gp + 2, :], in_=s["S"][:, 2 * gp:2 * gp + 2, :], func=AF.Copy), [L(("S", gp))], [L(("Sb", gp))])
                    if c + 1 < NCH:
                        bDs_next = head(c + 1)
                    for g in range(8):
                        act(lambda e, g=g: e.activation(out=xn[:, 0:256], in_=yz[:, g, :], func=AF.Square, scale=1.0 / 16, accum_out=ssq[:, g:g + 1]), [("yz", g // 2)], ["xn", ("ssq", g)])
                    dve(lambda e: e.tensor_scalar(out=ssq[:, :], in0=ssq[:, :], scalar1=EPS, scalar2=None, op0=ALU.add), [("ssq", g) for g in range(8)], [("ssq", g) for g in range(8)])
                    pool(lambda e: e.tensor_tensor(out=rstd8[:, :], in0=ssq[:, :], in1=neghalf[:, :], op=ALU.pow), [("ssq", g) for g in range(8)] + ["neghalf"], ["rstd8"])
                    for g in range(8):
                        act(lambda e, g=g: e.activation(out=yn[:, g, :], in_=yz[:, g, :], func=AF.Copy, scale=rstd8[:, g:g + 1]), ["rstd8", ("yz", g // 2)], ["yn"])

                    def tail(c=c):
                        ynf = yn[:, :, :].rearrange("p g n -> p (g n)")
                        for half in range(2):
                            b = bankS()
                            for kk in range(8):
                                k = half * 8 + kk
                                pe(lambda e, b=b, kk=kk, k=k: e.transpose(out=psb[b][:, kk * 128:(kk + 1) * 128], in_=ynf[:, k * 128:(k + 1) * 128], identity=ident[:]),
                                   ["yn", "ident"], [PK(b)])
                            act(lambda e, b=b, half=half: e.activation(out=ynT[:, half * 8:(half + 1) * 8, c * 128:(c + 1) * 128],
                                                                       in_=psb[b][:, :].rearrange("p (k t) -> p k t", k=8), func=AF.Copy),
                                [PK(b)], ["R2b"])
                    pending_tail = [tail]
                for f in pending_tail:
                    f()
                P.mark("t%d_l%d_C" % (it, l))
                for h2 in range(2):
                    proj_fm(l, C_GA + h2 * 512, lambda b, jj, h2=h2: act(
                        lambda e: e.activation(out=sga[:, h2 * 4 + jj, :], in_=ps[b][:, 0:T], func=AF.Sigmoid), [PK(b)], ["R1a"]))
                for h2 in range(2):
                    proj_fm(l, C_GB + h2 * 512, lambda b, jj, h2=h2: act(
                        lambda e: e.activation(out=sgb[:, h2 * 4 + jj, :], in_=ps[b][:, 0:T], func=AF.Sigmoid), [PK(b)], ["R1b"]))
                for h2 in range(2):
                    sl = wload(l, blk(l, "wa_bf", h2 * 512))
                    wv = wview(sl)
                    for jj in range(4):
                        j = h2 * 4 + jj
                        b = bank()
                        for k in range(8):
                            pe(lambda e, b=b, k=k, jj=jj, wv=wv: e.matmul(ps[b][:, 0:T], lhsT=wv[:, k, jj * 128:(jj + 1) * 128], rhs=yaT[:, k, :], start=(k == 0), stop=(k == 7)),
                               [("w", sl)] + [("yaT", g) for g in range(8)], [PK(b)])
                        dve(lambda e, b=b, j=j: e.tensor_tensor(out=sga[:, j, :], in0=ps[b][:, 0:T], in1=sga[:, j, :], op=ALU.mult), [PK(b), "R1a"], ["R1a"])
                for h2 in range(2):
                    sls = [wload(l, blk(l, "wb_bf", h2 * 512, r0=8 * kh)) for kh in range(2)]
                    for jj in range(4):
                        j = h2 * 4 + jj
                        b = bank()
                        for k in range(16):
                            wv = wview(sls[k // 8])
                            pe(lambda e, b=b, k=k, jj=jj, wv=wv: e.matmul(ps[b][:, 0:T], lhsT=wv[:, k % 8, jj * 128:(jj + 1) * 128], rhs=ynT[:, k, :], start=(k == 0), stop=(k == 15)),
                               [("w", sls[k // 8]), "R2b"], [PK(b)])
                        dve(lambda e, b=b, j=j: e.tensor_tensor(out=sgb[:, j, :], in0=ps[b][:, 0:T], in1=sgb[:, j, :], op=ALU.mult), [PK(b), "R1b"], ["R1b"])
                        dve(lambda e, j=j: e.tensor_tensor(out=mT[:, j, :], in0=sgb[:, j, :], in1=sga[:, j, :], op=ALU.add), ["R1a", "R1b"], HT_ALL + [("mT", j)])
                slo = [wload(l, blk(l, "wo_bf", h2 * 512)) for h2 in range(2)]
                for c in range(NCH):
                    for h2 in range(2):
                        wv = wview(slo[h2])
                        b = bank()
                        for k in range(8):
                            pe(lambda e, b=b, k=k, c=c, wv=wv: e.matmul(ps[b][:, :], lhsT=mT[:, k, c * 128:(c + 1) * 128], rhs=wv[:, k, :], start=(k == 0), stop=(k == 7)),
                               [("w", slo[h2])] + [("mT", j) for j in range(8)] + HT_ALL, [PK(b)])
                        tb = tmpy[h2]
                        tbf = tb[:, :, :].rearrange("p h q -> p (h q)")
                        dve(lambda e, b=b, tbf=tbf, h2=h2, s=s: e.tensor_tensor(out=tbf, in0=ps[b][:, :], in1=s["gate"][:, h2 * 512:(h2 + 1) * 512], op=ALU.mult),
                            [PK(b), L("gate")], [("tmpy", 0)])
                        dve(lambda e, tbf=tbf, c=c, h2=h2: e.tensor_tensor(out=xt[:, c, h2 * 512:(h2 + 1) * 512], in0=xt[:, c, h2 * 512:(h2 + 1) * 512], in1=tbf, op=ALU.add),
                            [("tmpy", 0), ("xt", c)], [("xt", c)])
                    if l == layers[-1]:
                        emit_out_chunk(si, it, c, dst_out)
        if dump:
            dbgS_d = nc.dram_tensor("dbg", [128, 2048], F32, kind="ExternalOutput").ap()
            P.dma("act", lambda e: e.dma_start(out=dbgS_d, in_=LS[0]["S"][:, :, :].rearrange("p g n -> p (g n)")),
                  [(("S", gp), 0) for gp in range(4)], [], semkey="dbgout")
        if debug is not None:
            if isinstance(debug, str):
                n = dict(P.marks)[debug]
            else:
                n = debug
            print("MARKS", P.marks, "total", len(P.order), "truncate at", n)
            P.truncate(n)
            P.dma("act", lambda e: e.dma_start(out=out_d[S - T:S, :].rearrange("(c p) n -> p c n", p=128), in_=xt[:, :, :]),
                  [("xt", c) for c in range(NCH)], [], semkey="xout_dbg")
        P.emit(nc)
    return nc


def _layer_maps(inp, l):
    f = lambda a: np.ascontiguousarray(a, dtype=np.float32)
    col = lambda v: f(np.asarray(v).reshape(-1, 128).T)
    bc = lambda v: f(np.broadcast_to(np.asarray(v).reshape(1, -1), (128, np.asarray(v).size)))
    m = {}
    m["ada_w_%d" % l] = f(inp["ada_w"][l])
    m["adab_col_%d" % l] = col(inp["ada_b"][l])
    m["adab_gate_bc_%d" % l] = bc(inp["ada_b"][l][2048:3072])
    m["normw_col_%d" % l] = col(inp["norm_w"][l])
    m["w_in_%d" % l] = f(inp["w_in"][l])
    m["lnw_bc_%d" % l] = bc(inp["gm_ln_w"][l])
    m["lnb_bc_%d" % l] = bc(inp["gm_ln_b"][l])
    m["gm_ws_%d" % l] = f(inp["gm_ws"][l])
    m["bs_row_%d" % l] = f(np.asarray(inp["gm_bs"][l]).reshape(1, 1024))
    m["convw_col_%d" % l] = f(np.asarray(inp["conv_w"][l]).reshape(4, 32, 128).transpose(2, 0, 1))
    m["convb_col_%d" % l] = col(inp["conv_b"][l])
    m["convb_row_%d" % l] = f(np.asarray(inp["conv_b"][l]).reshape(1, 4096))
    m["dtbias_bc_%d" % l] = bc(inp["dt_bias"][l])
    m["alog_bc_%d" % l] = bc(inp["a_log"][l])
    m["dskip_bc_%d" % l] = bc(inp["d_skip"][l])
    m["ssmnw_col_%d" % l] = col(inp["ssm_norm_w"][l])
    m["w_proj_a_%d" % l] = f(inp["w_proj_a"][l])
    m["w_proj_b_%d" % l] = f(inp["w_proj_b"][l])
    m["w_out_%d" % l] = f(inp["w_out"][l])
    return m


_NC_CACHE = {}


def _get_nc(S, NL, NCH, do_final):
    key = (S, NL, NCH, do_final)
    if key not in _NC_CACHE:
        _NC_CACHE[key] = build(S, NL, NCH, do_final)
    return _NC_CACHE[key]


def run_layers(x, inp, layers, do_final, NCH=4, n_cores=8):
    B, S, _ = x.shape
    nc = _get_nc(S, len(layers), NCH, do_final)
    in_maps = []
    for core in range(n_cores):
        b = core % B
        m = {"x": np.ascontiguousarray(x[b], dtype=np.float32),
             "c_col": np.ascontiguousarray(np.asarray(inp["c"][b]).reshape(8, 128).T, dtype=np.float32),
             "fnw_bc": np.ascontiguousarray(np.broadcast_to(np.asarray(inp["final_norm_w"]).reshape(1, 1024), (128, 1024)), dtype=np.float32)}
        for i, l in enumerate(layers):
            lm = _layer_maps(inp, l)
            for k, v in lm.items():
                base = k.rsplit("_", 1)[0]
                m["%s_%d" % (base, i)] = v
        in_maps.append(m)
    res = run_bass_kernel_spmd(nc, in_maps, core_ids=list(range(n_cores)))
    return np.stack([res.results[b]["out"] for b in range(B)], axis=0)


def kernel(**inputs):
    inp = {k: np.asarray(v) for k, v in inputs.items()}
    x = inp["x"]
    out = run_layers(x, inp, [0, 1], True)
    return out.astype(np.float32)
```

```python
import types
import numpy as np
from contextlib import ExitStack
import concourse.bass as bass
import concourse.mybir as mybir
from concourse.bass_utils import run_bass_kernel_spmd

F32 = mybir.dt.float32
BF16 = mybir.dt.bfloat16
AF = mybir.ActivationFunctionType
ALU = mybir.AluOpType

ENG_NAMES = ("pe", "act", "dve", "pool", "sp")
EPS = 1e-6


class Op:
    __slots__ = ("eng", "fn", "deps", "signal", "semval", "is_dma", "semkey", "dmacount", "pos")

    def __init__(self, eng, fn, is_dma=False, semkey=None):
        self.eng = eng
        self.fn = fn
        self.deps = []
        self.signal = False
        self.semval = 0
        self.is_dma = is_dma
        self.semkey = semkey
        self.dmacount = 0
        self.pos = 0


def _freeze(fn):
    if fn.__closure__ is None:
        return fn
    cells = []
    for c in fn.__closure__:
        try:
            cells.append(types.CellType(c.cell_contents))
        except ValueError:
            cells.append(c)
    g = types.FunctionType(fn.__code__, fn.__globals__, fn.__name__, fn.__defaults__, tuple(cells))
    g.__kwdefaults__ = fn.__kwdefaults__
    return g


class Prog:
    def __init__(self, same_engine_sync=True):
        self.ops = {e: [] for e in ENG_NAMES}
        self.lastw = {}
        self.readers = {}
        self.dma_counts = {}
        self.same_engine_sync = same_engine_sync
        self.order = []
        self.marks = []
        self.kept = None

    def mark(self, name):
        self.marks.append((name, len(self.order)))

    def truncate(self, n):
        keep = set(id(o) for o in self.order[:n])
        for e in ENG_NAMES:
            self.ops[e] = [o for o in self.ops[e] if id(o) in keep]
        self.order = self.order[:n]
        self.kept = keep
        self.dma_counts = {}
        for o in self.order:
            if o.is_dma:
                self.dma_counts[o.semkey] = max(self.dma_counts.get(o.semkey, 0), o.dmacount)

    def _deps(self, op, reads, writes):
        psr = [k for k in reads if isinstance(k, tuple) and k[0] == "ps"]
        if psr:
            reads = [k for k in reads if k not in psr]
            writes = list(writes) + psr
        deps = []
        for k in list(reads) + list(writes):
            w = self.lastw.get(k)
            if w is not None:
                deps.append(w)
        for k in writes:
            deps.extend(self.readers.get(k, ()))
        seen = set()
        out = []
        for d in deps:
            if id(d) in seen or d is op:
                continue
            seen.add(id(d))
            out.append(d)
        if self.kept is not None:
            out = [d for d in out if id(d) in self.kept]
        latest = {}
        rest = []
        for d in out:
            if d.is_dma:
                rest.append(d)
            elif d.eng not in latest or d.pos > latest[d.eng].pos:
                latest[d.eng] = d
        op.deps = rest + list(latest.values())
        for k in reads:
            self.readers.setdefault(k, []).append(op)
        for k in writes:
            self.lastw[k] = op
            self.readers[k] = []

    def add(self, eng, fn, reads=(), writes=()):
        op = Op(eng, _freeze(fn))
        op.pos = len(self.ops[eng])
        self._deps(op, reads, writes)
        self.ops[eng].append(op)
        self.order.append(op)
        return op

    def dma(self, eng, fn, reads=(), writes=(), semkey=None):
        op = Op(eng, _freeze(fn), is_dma=True, semkey=semkey)
        op.pos = len(self.ops[eng])
        self._deps(op, reads, writes)
        c = self.dma_counts.get(semkey, 0) + 16
        self.dma_counts[semkey] = c
        op.dmacount = c
        self.ops[eng].append(op)
        self.order.append(op)
        return op

    def _skip(self, op, d):
        return (not d.is_dma) and (not op.is_dma) and d.eng == op.eng and (op.eng == "pe" or not self.same_engine_sync)

    def emit(self, nc):
        for e in ENG_NAMES:
            for op in self.ops[e]:
                for d in op.deps:
                    if d.is_dma or self._skip(op, d):
                        continue
                    d.signal = True
        for e in ENG_NAMES:
            c = 0
            for op in self.ops[e]:
                if op.signal and not op.is_dma:
                    c += 1
                    op.semval = c
        engs = {"pe": nc.tensor, "act": nc.scalar, "dve": nc.vector, "pool": nc.gpsimd, "sp": nc.sync}
        with ExitStack() as st:
            esem = {e: st.enter_context(nc.semaphore("s_" + e)) for e in ENG_NAMES}
            dsem = {k: st.enter_context(nc.semaphore("d_%d" % i)) for i, k in enumerate(self.dma_counts)}
            block = st.enter_context(nc.Block())

            def run(ename):
                eng = engs[ename]
                waited = {}
                for op in self.ops[ename]:
                    for d in op.deps:
                        if d.is_dma:
                            s, v = dsem[d.semkey], d.dmacount
                        else:
                            if self._skip(op, d):
                                continue
                            s, v = esem[d.eng], d.semval
                        if waited.get(id(s), 0) >= v:
                            continue
                        waited[id(s)] = v
                        eng.wait_ge(s, v)
                    ins = op.fn(eng)
                    if op.is_dma:
                        ins.then_inc(dsem[op.semkey], 16)
                    elif op.signal:
                        ins.then_inc(esem[ename], 1)
                fin = {}
                for op in self.ops[ename]:
                    if op.is_dma:
                        fin[op.semkey] = max(fin.get(op.semkey, 0), op.dmacount)
                for k, v in fin.items():
                    if waited.get(id(dsem[k]), 0) < v:
                        eng.wait_ge(dsem[k], v)

            @block.tensor
            def _(e):
                run("pe")

            @block.scalar
            def _(e):
                run("act")

            @block.vector
            def _(e):
                run("dve")

            @block.gpsimd
            def _(e):
                run("pool")

            @block.sync
            def _(e):
                run("sp")


C_U, C_V, C_Z, C_SZ, C_XBC, C_DT, C_GA, C_GB = 0, 1024, 2048, 3072, 5120, 9216, 9248, 10272
N_IN = 11296
NSLOT = 3
SAME_ENGINE_SYNC = True

LAYER_INPUTS = [
    ("ada_w", [1024, 3072]), ("adab_col", [128, 24]), ("adab_gate_bc", [128, 1024]), ("normw_col", [128, 8]),
    ("w_in", [1024, N_IN]), ("lnw_bc", [128, 1024]), ("lnb_bc", [128, 1024]), ("gm_ws", [8, 128, 128]),
    ("bs_row", [1, 1024]), ("convw_col", [128, 4, 32]), ("convb_col", [128, 32]), ("convb_row", [1, 4096]),
    ("dtbias_bc", [128, 32]), ("alog_bc", [128, 32]), ("dskip_bc", [128, 32]), ("ssmnw_col", [128, 16]),
    ("w_proj_a", [1024, 1024]), ("w_proj_b", [2048, 1024]), ("w_out", [1024, 1024]),
]


def build(S, NL, NCH, do_final, debug=None, dump=False, layer_major=True):
    T = NCH * 128
    NT = S // T
    assert S % T == 0
    nc = bass.Bass("TRN2", target_bir_lowering=False)
    P = Prog(same_engine_sync=SAME_ENGINE_SYNC)

    def din(name, shape):
        return nc.dram_tensor(name, shape, F32, kind="ExternalInput").ap()

    x_d = din("x", [S, 1024])
    c_d = din("c_col", [128, 8])
    fnw_d = din("fnw_bc", [128, 1024])
    out_d = nc.dram_tensor("out", [S, 1024], F32, kind="ExternalOutput").ap()
    LD = []
    for l in range(NL):
        d = {n: din("%s_%d" % (n, l), sh) for n, sh in LAYER_INPUTS}
        d["win_bf"] = nc.dram_tensor("win_bf_%d" % l, [1024, N_IN], BF16, kind="Internal").ap()
        d["wa_bf"] = nc.dram_tensor("wa_bf_%d" % l, [1024, 1024], BF16, kind="Internal").ap()
        d["wb_bf"] = nc.dram_tensor("wb_bf_%d" % l, [2048, 1024], BF16, kind="Internal").ap()
        d["wo_bf"] = nc.dram_tensor("wo_bf_%d" % l, [1024, 1024], BF16, kind="Internal").ap()
        d["diag_bf"] = nc.dram_tensor("diag_bf_%d" % l, [4, 128, 4096], BF16, kind="Internal").ap()
        LD.append(d)

    st = ExitStack()
    with st:
        def sb(name, shape, dt):
            return st.enter_context(nc.sbuf_tensor(name, shape, dt))

        xt = sb("xt", [128, NCH, 1024], F32)
        xn = sb("xn", [128, 1024], BF16)
        hT = sb("hT", [128, 8, T], BF16)
        R1 = sb("R1", [128, 32 * T], BF16)
        R2 = sb("R2", [128, 32 * (T + 3)], BF16)
        yaT = sb("yaT", [128, 8, T], BF16)
        CT = sb("CT", [128, 8, T], BF16)
        wsl = [sb("wsl%d" % i, [128, 4096], BF16) for i in range(NSLOT)]
        gu = R1[:, 0:8 * T].rearrange("p (j t) -> p j t", j=8)
        sz = R1[:, 8 * T:16 * T].rearrange("p (j t) -> p j t", j=8)
        gv = R1[:, 16 * T:24 * T].rearrange("p (c n) -> p c n", c=NCH)
        vn = R1[:, 24 * T:32 * T].rearrange("p (c n) -> p c n", c=NCH)
        xs_tok = R1[:, 0:16 * T].rearrange("p (c n) -> p c n", c=NCH)
        B_tok = R1[:, 16 * T:24 * T].rearrange("p (c n) -> p c n", c=NCH)
        BT = R1[:, 24 * T:32 * T].rearrange("p (j t) -> p j t", j=8)
        sga = gu
        sgb = sz
        xbcT = R2[:, :].rearrange("p (j t) -> p j t", j=32)
        szs = R2[:, 0:16 * T].rearrange("p (c n) -> p c n", c=NCH)
        ynT = R2[:, 16 * (T + 3):16 * (T + 3) + 16 * T].rearrange("p (j t) -> p j t", j=16)
        mT = hT
        NB2 = 2
        rhsD = [sb("rhsD%d" % i, [128, 4, 128], F32) for i in range(NB2)]
        expD = [sb("expD%d" % i, [128, 4, 128], BF16) for i in range(NB2)]
        Gt = [sb("G%d" % i, [128, 4, 128], BF16) for i in range(4)]
        CBm = sb("CBm", [128, 8, 128], BF16)
        xd = sb("xd", [128, 32, 64], BF16)
        xdte = sb("xdte", [128, 32, 64], BF16)
        xsD = sb("xsD", [128, 32, 64], BF16)
        yz = sb("yz", [128, 8, 256], F32)
        tmpy = [sb("tmpy0", [128, 8, 64], F32)] * 2
        yn = sb("yn", [128, 8, 256], BF16)
        junk = yn[:, :, :].rearrange("p g n -> p (g n)")[:, 0:1024]
        dtb = sb("dtb", [128, NCH, 32], F32)
        dtt = sb("dtt", [128, NCH, 32], F32)
        adt = sb("adt", [128, NCH, 32], F32)
        acs_a = sb("acs_a", [128, NCH, 32], F32)
        el_a = sb("el_a", [128, NCH, 32], F32)
        dec_a = sb("dec_a", [128, NCH, 32], F32)
        dd_a = sb("dd_a", [128, NCH, 32], F32)
        dtdte_a = sb("dtdte_a", [128, NCH, 32], F32)
        ssq = sb("ssq", [128, 8], F32)
        rstd8 = sb("rstd8", [128, 8], F32)
        st1 = sb("st1", [128, 8], F32)
        stp = sb("stp", [128, NCH, 4], F32)
        bnst = sb("bnst", [128, 2, 6], F32)
        bnag = sb("bnag", [128, 2], F32)
        ident = sb("ident", [128, 128], BF16)
        identf = sb("identf", [128, 128], F32)
        triu = sb("triu", [128, 128], F32)
        mgt = sb("mgt", [128, 128], F32)
        onesf = sb("onesf", [128, 128], F32)
        ones_row = sb("ones_row", [128, 128], BF16)
        sel3 = sb("sel3", [128, 3, 128], BF16)
        fnw = sb("fnw", [128, 1024], F32) if do_final else None
        ccol = sb("ccol", [128, 8], F32)
        sc2 = sb("sc2", [128, 8, 2], F32)
        R2f = R2[:, :].bitcast(F32)
        screp = R2f[:, 0:1024].rearrange("p (k m) -> p k m", k=8)
        modc = sb("modc", [128, 32], F32)
        neghalf = sb("neghalf", [128, 8], F32)
        LS = []
        for l in range(1 if layer_major else NL):
            s = {}
            s["S"] = sb("S_%d" % l, [128, 8, 256], F32)
            s["Sb"] = sb("Sb_%d" % l, [128, 8, 256], BF16)
            s["lnw"] = sb("lnw_%d" % l, [128, 1024], BF16)
            s["lnb"] = sb("lnb_%d" % l, [128, 1024], BF16)
            s["WsT"] = sb("WsT_%d" % l, [128, 8, 128], BF16)
            s["brow"] = sb("brow_%d" % l, [128, 1024], BF16)
            s["crow"] = sb("crow_%d" % l, [128, 1024], BF16)
            s["convb"] = sb("convb_%d" % l, [128, 32], F32)
            s["dtbias"] = sb("dtbias_%d" % l, [128, 32], F32)
            s["a_bc"] = sb("abc_%d" % l, [128, 32], F32)
            s["dskip"] = sb("dskip_%d" % l, [128, 32], F32)
            s["gate"] = sb("gate_%d" % l, [128, 1024], BF16)
            s["weff"] = sb("weff_%d" % l, [128, 8], F32)
            s["shift"] = sb("shift_%d" % l, [128, 8], F32)
            s["halo"] = sb("halo_%d" % l, [128, 32, 3], BF16)
            s["wdt"] = sb("wdt_%d" % l, [128, 8, 32], BF16)
            LS.append(s)
        if layer_major:
            LS = LS * NL
        x1_d = nc.dram_tensor("x1_scr", [S, 1024], F32, kind="Internal").ap() if (layer_major and NL > 1) else None
        stage = {"adabc": sb("adabc", [128, 24], F32), "nwc": sb("nwc", [128, 8], F32), "cwc": sb("cwc", [128, 4, 32], F32), "alog": sb("alog", [128, 32], F32), "ssw": sb("ssw", [128, 16], F32)}
        ps = [st.enter_context(nc.psum_tensor("ps%d" % i, [128, 512], F32)) for i in range(8)]
        psb = [p[:, :].bitcast(BF16) for p in ps]

        dbg_slots = {}
        dbg_d = nc.dram_tensor("dbg2", [8, 128, 512], F32, kind="ExternalOutput").ap() if dump else None

        def DBG(tag, ap, keys):
            if not dump or tag not in dump or len(dbg_slots) >= 8:
                return
            i = len(dbg_slots)
            dbg_slots[tag] = i
            P.dma("pool", lambda e: e.dma_start(out=dbg_d[i], in_=ap), keys, [], semkey=("dbg2", i))

        bank_ctr = [0]

        def bank():
            i = bank_ctr[0] % 8
            bank_ctr[0] += 1
            return i

        def PK(i):
            return ("ps", i)

        act = lambda fn, r, w: P.add("act", fn, r, w)
        dve = lambda fn, r, w: P.add("dve", fn, r, w)
        pool = lambda fn, r, w: P.add("pool", fn, r, w)
        pe = lambda fn, r, w: P.add("pe", fn, r, w)

        pool(lambda e: e.memset(identf[:], 1.0), [], ["identf"])
        pool(lambda e: e.affine_select(out=identf[:], in_=identf[:], pattern=[[-1, 128]], compare_op=ALU.is_equal,
                                       fill=0.0, base=0, channel_multiplier=1), ["identf"], ["identf"])
        pool(lambda e: e.memset(onesf[:], 1.0), [], ["onesf"])
        pool(lambda e: e.affine_select(out=triu[:], in_=onesf[:], pattern=[[1, 128]], compare_op=ALU.is_ge,
                                       fill=0.0, base=0, channel_multiplier=-1), ["onesf"], ["triu"])
        pool(lambda e: e.affine_select(out=mgt[:], in_=onesf[:], pattern=[[-1, 128]], compare_op=ALU.is_gt,
                                       fill=0.0, base=0, channel_multiplier=1), ["onesf"], ["mgt"])
        dve(lambda e: e.tensor_copy(out=ident[:], in_=identf[:]), ["identf"], ["ident"])
        dve(lambda e: e.memset(ones_row[:], 0.0), [], ["ones_row"])
        dve(lambda e: e.memset(ones_row[0:2, :], 1.0), ["ones_row"], ["ones_row"])
        dve(lambda e: e.memset(neghalf[:], -0.5), [], ["neghalf"])
        dve(lambda e: e.memset(sel3[:], 1.0), [], ["sel3"])
        pool(lambda e: e.affine_select(out=sel3[:], in_=sel3[:], pattern=[[-1, 3], [0, 128]], compare_op=ALU.is_equal, fill=0.0, base=0,
                                       channel_multiplier=1), ["sel3"], ["sel3"])
        if do_final:
            P.dma("act", lambda e: e.dma_start(out=fnw[:], in_=fnw_d), [], ["fnw"], semkey="c_fnw")
        P.dma("act", lambda e: e.dma_start(out=ccol[:], in_=c_d), [], ["ccol"], semkey="c_ccol")
        act(lambda e: e.activation(out=sc2[:, :, 0], in_=ccol[:], func=AF.Silu), ["ccol"], ["sc2"])
        act(lambda e: e.activation(out=sc2[:, :, 1], in_=ccol[:], func=AF.Silu), ["ccol"], ["sc2"])

        P.mark("consts")
        W = 128 * NCH
        for l in range(NL):
            d = LD[l]
            L = lambda n, l=l: (n, l)
            for q in range(6):
                c0_, c1_ = q * 2048, min((q + 1) * 2048, N_IN)
                P.dma("pool", lambda e, c0_=c0_, c1_=c1_, d=d: e.dma_start(out=d["win_bf"][:, c0_:c1_], in_=d["w_in"][:, c0_:c1_],
                                                                            max_dma_last_dim=4096),
                      [], [("wscrc", l, q), ("castorder", l)], semkey=("castc", l, q))
            P.dma("pool", lambda e, d=d: e.dma_start(out=d["wa_bf"], in_=d["w_proj_a"], max_dma_last_dim=4096), [], [L("wscr"), ("castorder", l)], semkey=L("castw"))
            P.dma("pool", lambda e, d=d: e.dma_start(out=d["wo_bf"], in_=d["w_out"], max_dma_last_dim=4096), [], [L("wscr"), ("castorder", l)], semkey=L("castw"))
            P.mark("castdma%d" % l)

        def setup_layer(l):
            d, s = LD[l], LS[l]
            lk = 0 if layer_major else l
            L = lambda n: ("wscr", l) if n == "wscr" else (n, lk)
            ld = lambda dst, src, key: P.dma("act", lambda e: e.dma_start(out=dst, in_=src), [], [key], semkey=("ld", key))
            adabc, nwc, cwc, alog = stage["adabc"], stage["nwc"], stage["cwc"], stage["alog"]
            ld(adabc[:], d["adab_col"], L("adabc"))
            ld(nwc[:], d["normw_col"], L("nwc"))
            ld(cwc[:], d["convw_col"], L("cwc"))
            ld(alog[:], d["alog_bc"], L("alog"))
            ld(s["convb"][:], d["convb_col"], L("convb"))
            ld(s["dtbias"][:], d["dtbias_bc"], L("dtbias"))
            ld(s["dskip"][:], d["dskip_bc"], L("dskip"))
            P.dma("pool", lambda e, s=s, d=d: e.dma_start(out=s["lnw"][:], in_=d["lnw_bc"]), [], [L("lnw")], semkey=L("c_lnw"))
            P.dma("pool", lambda e, s=s, d=d: e.dma_start(out=s["lnb"][:], in_=d["lnb_bc"]), [], [L("lnb")], semkey=L("c_lnb"))
            P.mark("smallld%d" % l)
            ssw = stage["ssw"]
            ld(ssw[:], d["ssmnw_col"], L("ssw"))
            for kc in range(16):
                pf, pb = kc % 4, kc % 2
                stf = R2f[:, 4096 + pf * 1024:4096 + (pf + 1) * 1024]
                stb = R2[:, 6144 + pb * 1024:6144 + (pb + 1) * 1024]
                P.dma("act", lambda e, d=d, kc=kc, stf=stf: e.dma_start(out=stf, in_=d["w_proj_b"][kc * 128:(kc + 1) * 128, :]),
                      [], [("wbf", pf)] + (["R2a", "R2b"] if kc < 4 else []), semkey=L(("ld_wb", pf)))
                dve(lambda e, kc=kc, stf=stf, stb=stb: e.tensor_scalar(out=stb, in0=stf, scalar1=ssw[:, kc:kc + 1], scalar2=None, op0=ALU.mult),
                    [("wbf", pf), L("ssw")], [("wbb", pb)])
                P.dma("act", lambda e, d=d, kc=kc, stb=stb: e.dma_start(out=d["wb_bf"][kc * 128:(kc + 1) * 128, :], in_=stb),
                      [("wbb", pb)] + (["R2a", "R2b"] if kc >= 14 else []), [("wscr", l)], semkey=L(("st_wb", pb)))
            dve(lambda e: e.tensor_copy(out=screp, in_=sc2[:, :, 0:1].broadcast_to([128, 8, 128])), ["sc2"], ["screp", "R2a", "R2b"])
            P.dma("act", lambda e, d=d: e.dma_start(out=R2f[:, 1024:2048], in_=d["adab_gate_bc"]), [], ["R2a", "R2b"], semkey=L("ld_gate"))
            adaw_t = xt[:, :, :].rearrange("p c n -> p (c n)")
            adaw_v = adaw_t.rearrange("p (k n) -> p k n", k=8)
            bm = bank()
            for q in range(3072 // W):
                c0 = q * W
                P.dma("sp", lambda e, c0=c0, d=d: e.dma_start(out=adaw_v, in_=d["ada_w"].rearrange("(k p) n -> p k n", p=128)[:, :, c0:c0 + W]),
                      [], ["xt_all"] + [("xt", c_) for c_ in range(NCH)], semkey="adaw")
                if c0 < 2048:
                    for jj in range(NCH):
                        jidx = (c0 // 128) + jj
                        for k in range(8):
                            pe(lambda e, jidx=jidx, jj=jj, k=k, bm=bm: e.matmul(ps[bm][:, 2 * jidx:2 * jidx + 2],
                                                                                  lhsT=adaw_v[:, k, jj * 128:(jj + 1) * 128], rhs=sc2[:, k, :],
                                                                                  start=(k == 0), stop=(k == 7)),
                               ["xt_all", "sc2"], [PK(bm)])
                else:
                    bg = bank()
                    for k in range(8):
                        pe(lambda e, k=k, bg=bg: e.matmul(ps[bg][:, 0:W], lhsT=screp[:, k, :], rhs=adaw_v[:, k, :],
                                                          start=(k == 0), stop=(k == 7)),
                           ["xt_all", "screp", "R2a", "R2b"], [PK(bg)])
                    g0 = c0 - 2048
                    dve(lambda e, bg=bg, g0=g0, s=s: e.tensor_tensor(out=s["gate"][:, g0:g0 + W], in0=ps[bg][:, 0:W], in1=R2f[:, 1024 + g0:1024 + g0 + W], op=ALU.add),
                        [PK(bg), "R2a", "R2b"], [L("gate")])
            dve(lambda e, bm=bm: e.tensor_tensor(out=modc[:, 0:16], in0=ps[bm][:, 0:32].rearrange("p (j two) -> p j two", two=2)[:, :, 0],
                                                 in1=adabc[:, 0:16], op=ALU.add), [PK(bm), L("adabc")], ["modc"])
            dve(lambda e, s=s: e.tensor_copy(out=s["shift"][:], in_=modc[:, 0:8]), ["modc"], [L("shift")])
            dve(lambda e, s=s, nwc=nwc: e.scalar_tensor_tensor(out=s["weff"][:], in0=modc[:, 8:16], scalar=1.0, in1=nwc[:], op0=ALU.add, op1=ALU.mult),
                ["modc", L("nwc")], [L("weff")])
            P.mark("adaln%d" % l)
            wst_f = yz[:, :, :].rearrange("p g n -> p (g n)")[:, 0:1024].rearrange("p (g s) -> p g s", g=8)
            P.dma("act", lambda e, d=d: e.dma_start(out=wst_f, in_=d["gm_ws"].rearrange("g t s -> t g s")), [], ["yz"], semkey="ld_ws")
            pool(lambda e: e.affine_select(out=wst_f, in_=wst_f, pattern=[[0, 8], [-1, 128]], compare_op=ALU.is_ge, fill=0.0, base=0,
                                           channel_multiplier=1), ["yz"], ["yz"])
            dve(lambda e: e.tensor_copy(out=junk[:, :].rearrange("p (g s) -> p g s", g=8), in_=wst_f), ["yz"], ["yn"])
            bw = bank()
            for g in range(8):
                pe(lambda e, g=g, bw=bw: e.transpose(out=psb[bw][:, g * 128:(g + 1) * 128], in_=junk[:, g * 128:(g + 1) * 128], identity=ident[:]),
                   ["yn", "ident"], [PK(bw)])
            dve(lambda e, bw=bw, s=s: e.tensor_copy(out=s["WsT"][:].rearrange("p g t -> p (g t)"), in_=psb[bw][:, 0:1024]), [PK(bw)], [L("WsT")])
            P.mark("wst%d" % l)
            P.dma("act", lambda e, d=d: e.dma_start(out=R2f[0:3, 0:1024], in_=d["convb_row"][:, 0:3072].rearrange("o (i n) -> (o i) n", i=3)),
                  ["screp"], ["R2a", "R2b", L("cbr")], semkey=L("ld_cbr"))
            P.dma("act", lambda e, d=d: e.dma_start(out=R2f[0:1, 1024:2048], in_=d["bs_row"]), ["screp"], ["R2a", "R2b", L("bsr")], semkey=L("ld_bsr"))
            pool(lambda e, s=s: e.memset(s["crow"][:, :], 0.0), [], [L("cbrhi")])
            pool(lambda e, s=s: e.memset(s["brow"][:, :], 0.0), [], [L("bsrhi"), L("bsrlo")])
            dve(lambda e, s=s: e.tensor_copy(out=s["crow"][0:3, :], in_=R2f[0:3, 0:1024]), [L("cbr"), "R2a", "R2b"], [L("cbrhi")])
            hi_b = R2[0:1, 4096:5120]
            lo_b = R2[0:1, 5120:6144]
            dve(lambda e, hi_b=hi_b: e.tensor_copy(out=hi_b, in_=R2f[0:1, 1024:2048]), [L("bsr"), "R2a", "R2b"], ["R2a", "R2b", L("hib")])
            dve(lambda e, hi_b=hi_b, lo_b=lo_b: e.tensor_tensor(out=lo_b, in0=R2f[0:1, 1024:2048], in1=hi_b, op=ALU.subtract), [L("bsr"), L("hib"), "R2a", "R2b"], ["R2a", "R2b", L("lob")])
            P.dma("act", lambda e, s=s, hi_b=hi_b: e.dma_start(out=s["brow"][0:1, :], in_=hi_b), [L("hib"), "R2a", "R2b"], [L("bsrhi")], semkey=L("mv_hi"))
            P.dma("act", lambda e, s=s, lo_b=lo_b: e.dma_start(out=s["brow"][1:2, :], in_=lo_b), [L("lob"), "R2a", "R2b"], [L("bsrlo")], semkey=L("mv_lo"))
            act(lambda e, s=s, alog=alog: e.activation(out=s["a_bc"][:], in_=alog[:], func=AF.Exp), [L("alog")], [L("a_bc")])
            dve(lambda e, s=s: e.tensor_scalar(out=s["a_bc"][:], in0=s["a_bc"][:], scalar1=-1.0, scalar2=None, op0=ALU.mult), [L("a_bc")], [L("a_bc")])
            P.mark("rows%d" % l)
            for q in range(4):
                stg = wsl[q % NSLOT][:, :].rearrange("p (j k n) -> p j k n", j=8, k=4)
                cw_q = cwc[:, :, 8 * q:8 * q + 8].rearrange("p k j -> p j k").unsqueeze(3).broadcast_to([128, 8, 4, 128])
                id_q = identf[:, :].unsqueeze(1).unsqueeze(1).broadcast_to([128, 8, 4, 128])
                dve(lambda e, stg=stg, cw_q=cw_q, id_q=id_q: e.tensor_tensor(out=stg, in0=id_q, in1=cw_q, op=ALU.mult),
                    ["identf", L("cwc")], [("w", q % NSLOT)])
                P.dma("sp", lambda e, q=q, d=d: e.dma_start(out=d["diag_bf"][q], in_=wsl[q % NSLOT][:, :]), [("w", q % NSLOT)], [L("wscr")], semkey=L("diagst"))
            P.dma("sp", lambda e, s=s, d=d: e.dma_start(out=s["wdt"][:], in_=d["win_bf"].rearrange("(k p) n -> p k n", p=128)[:, :, C_DT:C_DT + 32]),
                  win_parts(l, C_DT, 32), [L("wdt")], semkey=L("ld_wdt"))
            pool(lambda e, s=s: e.memset(s["S"][:], 0.0), [], [L(("S", gp)) for gp in range(4)])
            pool(lambda e, s=s: e.memset(s["Sb"][:], 0.0), [], [L(("Sb", gp)) for gp in range(4)])
            pool(lambda e, s=s: e.memset(s["halo"][:], 0.0), [], [L("halo")])

        if not layer_major:
            for l in range(NL):
                setup_layer(l)
        P.mark("setup_done")
        wctr = [0]

        def win_parts(l, c0, width=512):
            return [("wscrc", l, q) for q in range(c0 // 2048, (c0 + width - 1) // 2048 + 1)]

        def wload(l, src_ap_fn, first=False, extra=()):
            sl = wctr[0] % NSLOT
            wctr[0] += 1
            P.dma("sp", lambda e: e.dma_start(out=wsl[sl][:, :].rearrange("p (k n) -> p k n", k=8), in_=src_ap_fn()),
                  list(extra) if extra else [("wscr", l)], [("w", sl)], semkey=("wsl", sl))
            return sl

        def wload_raw(l, src_ap_fn):
            sl = wctr[0] % NSLOT
            wctr[0] += 1
            P.dma("sp", lambda e: e.dma_start(out=wsl[sl][:, :], in_=src_ap_fn()), [("wscr", l)], [("w", sl)], semkey=("wsl", sl))
            return sl

        def wview(sl):
            return wsl[sl][:, :].rearrange("p (k n) -> p k n", k=8)

        def blk(l, name, c0, r0=0):
            return lambda: LD[l][name].rearrange("(k p) n -> p k n", p=128)[:, r0:r0 + 8, c0:c0 + 512]

        HT_ALL = [("hT", c) for c in range(NCH)]

        def XS(c):
            return "R1a" if c < max(NCH // 2, 1) else "R1b"

        def proj_fm(l, c0, evac):
            sl = wload(l, blk(l, "win_bf", c0), extra=win_parts(l, c0))
            wv = wview(sl)
            for jj in range(4):
                b = bank()
                for k in range(8):
                    pe(lambda e, b=b, k=k, jj=jj, wv=wv: e.matmul(ps[b][:, 0:T], lhsT=wv[:, k, jj * 128:(jj + 1) * 128], rhs=hT[:, k, :],
                                                                   start=(k == 0), stop=(k == 7)),
                       [("w", sl)] + HT_ALL, [PK(b)])
                evac(b, jj)

        def proj_tm(l, c0, evac):
            sl = wload(l, blk(l, "win_bf", c0), extra=win_parts(l, c0))
            wv = wview(sl)
            for c in range(NCH):
                b = bank()
                for k in range(8):
                    pe(lambda e, b=b, k=k, c=c, wv=wv: e.matmul(ps[b][:, :], lhsT=hT[:, k, c * 128:(c + 1) * 128], rhs=wv[:, k, :],
                                                                 start=(k == 0), stop=(k == 7)),
                       [("w", sl), ("hT", c)], [PK(b)])
                evac(b, c)

        if layer_major:
            schedule = [(it, [l], l == 0, l == NL - 1, [l] if it == 0 else []) for l in range(NL) for it in range(NT)]
        else:
            schedule = [(it, list(range(NL)), True, True, []) for it in range(NT)]
        def emit_xload(src_d, src_x, it, c, after_setup=False):
            r0 = it * T + c * 128
            P.dma("pool", lambda e: e.dma_start(out=xt[:, c, :], in_=src_d[r0:r0 + 128, :]),
                  [] if src_x else [("x1", it, c)], [("xt", c)] + (["xt_all"] if after_setup else []), semkey=("xin", c))

        def emit_out_chunk(si, it, c, dst_out):
            nonlocal preloaded
            if do_final and dst_out:
                act(lambda e: e.activation(out=junk[:, :], in_=xt[:, c, :], func=AF.Square, scale=1.0 / 32, accum_out=st1[:, 0:1]), [("xt", c)], ["yn", "st1"])
                dve(lambda e: e.tensor_scalar(out=st1[:, 1:2], in0=st1[:, 0:1], scalar1=EPS, scalar2=None, op0=ALU.add), ["st1"], ["st1b"])
                pool(lambda e: e.tensor_tensor(out=st1[:, 2:3], in0=st1[:, 1:2], in1=neghalf[:, 0:1], op=ALU.pow), ["st1b", "neghalf"], ["st1c"])
                dve(lambda e: e.scalar_tensor_tensor(out=xt[:, c, :], in0=xt[:, c, :], scalar=st1[:, 2:3], in1=fnw[:, :], op0=ALU.mult, op1=ALU.mult),
                    ["st1c", ("xt", c), "fnw"], [("xt", c)])
            dst_d = out_d if dst_out else x1_d
            r0 = it * T + c * 128
            P.dma("act", lambda e: e.dma_start(out=dst_d[r0:r0 + 128, :], in_=xt[:, c, :]), [("xt", c)], [] if dst_out else [("x1", it, c)], semkey=("xout", c))
            if si + 1 < len(schedule) and not schedule[si + 1][4]:
                (it2, _, src_x2, _, _) = schedule[si + 1]
                emit_xload(x_d if src_x2 else x1_d, src_x2, it2, c)
                if c == NCH - 1:
                    preloaded = True

        preloaded = False
        for si, (it, layers, src_x, dst_out, setups) in enumerate(schedule):
            t0 = it * T
            for l_ in setups:
                setup_layer(l_)
            src_d = x_d if src_x else x1_d
            if not preloaded:
                for c in range(NCH):
                    emit_xload(src_d, src_x, it, c, after_setup=True)
            preloaded = False
            for l in layers:
                d, s = LD[l], LS[l]
                lk = 0 if layer_major else l
                L = lambda n, lk=lk: (n, lk)
                P.mark("t%d_l%d_start" % (it, l))
                for c in range(NCH):
                    act(lambda e, c=c: e.activation(out=junk[:, :], in_=xt[:, c, :], func=AF.Square, scale=1.0 / 32, accum_out=stp[:, c, 0:1]),
                        [("xt", c)], ["yn", ("stp", c)])
                for c in range(NCH):
                    dve(lambda e, c=c: e.tensor_scalar(out=stp[:, c, 1:2], in0=stp[:, c, 0:1], scalar1=EPS, scalar2=None, op0=ALU.add), [("stp", c)], [("stpb", c)])
                    pool(lambda e, c=c: e.tensor_tensor(out=stp[:, c, 2:3], in0=stp[:, c, 1:2], in1=neghalf[:, 0:1], op=ALU.pow), [("stpb", c), "neghalf"], [("stpc", c)])
                for c in range(NCH):
                    dve(lambda e, c=c: e.tensor_scalar(out=xn[:, :], in0=xt[:, c, :], scalar1=stp[:, c, 2:3], scalar2=None, op0=ALU.mult),
                        [("stpc", c), ("xt", c)], ["xn"])
                    b = bank()
                    for k in range(8):
                        pe(lambda e, b=b, k=k: e.transpose(out=psb[b][:, k * 128:(k + 1) * 128], in_=xn[:, k * 128:(k + 1) * 128], identity=ident[:]),
                           ["xn", "ident"], [PK(b)])
                    for k in range(8):
                        act(lambda e, b=b, k=k, c=c, s=s: e.activation(out=hT[:, k, c * 128:(c + 1) * 128], in_=psb[b][:, k * 128:(k + 1) * 128], func=AF.Identity,
                                                                        scale=s["weff"][:, k:k + 1], bias=s["shift"][:, k:k + 1]),
                            [PK(b), L("weff"), L("shift")], [("hT", c)])
                P.mark("t%d_l%d_A" % (it, l))
                for h2 in range(2):
                    proj_fm(l, C_U + h2 * 512, lambda b, jj, h2=h2: act(
                        lambda e: e.activation(out=gu[:, h2 * 4 + jj, :], in_=ps[b][:, 0:T], func=AF.Gelu_apprx_tanh), [PK(b)], ["R1a"]))
                for h2 in range(2):
                    proj_fm(l, C_Z + h2 * 512, lambda b, jj, h2=h2: act(
                        lambda e: e.activation(out=sz[:, h2 * 4 + jj, :], in_=ps[b][:, 0:T], func=AF.Silu), [PK(b)], ["R1b"]))
                pool(lambda e: e.tensor_tensor(out=gu[:, :, :], in0=gu[:, :, :], in1=sz[:, :, :], op=ALU.mult), ["R1a", "R1b"], ["R1a"])
                for h2 in range(2):
                    proj_tm(l, C_V + h2 * 512, lambda b, c, h2=h2: act(
                        lambda e: e.activation(out=gv[:, c, h2 * 512:(h2 + 1) * 512], in_=ps[b][:, :], func=AF.Gelu_apprx_tanh), [PK(b)], [("R1c", c)]))
                for c in range(NCH):
                    for h2 in range(2):
                        dve(lambda e, c=c, h2=h2: e.bn_stats(out=bnst[:, h2, :], in_=gv[:, c, h2 * 512:(h2 + 1) * 512]), [("R1c", c)], ["bnst"])
                    dve(lambda e: e.bn_aggr(out=bnag[:, :], in_=bnst[:, :, :].rearrange("p a b -> p (a b)")), ["bnst"], ["bnag"])
                    dve(lambda e: e.tensor_scalar(out=st1[:, 3:4], in0=bnag[:, 1:2], scalar1=EPS, scalar2=None, op0=ALU.add), ["bnag"], ["st1d"])
                    pool(lambda e: e.tensor_tensor(out=st1[:, 4:5], in0=st1[:, 3:4], in1=neghalf[:, 0:1], op=ALU.pow), ["st1d", "neghalf"], ["st1e"])
                    dve(lambda e, c=c, s=s: e.scalar_tensor_tensor(out=junk[:, :], in0=gv[:, c, :], scalar=bnag[:, 0:1], in1=s["lnw"][:, :], op0=ALU.subtract, op1=ALU.mult),
                        [("R1c", c), "bnag", L("lnw")], ["yn"])
                    dve(lambda e, c=c, s=s: e.scalar_tensor_tensor(out=vn[:, c, :], in0=junk[:, :], scalar=st1[:, 4:5], in1=s["lnb"][:, :], op0=ALU.mult, op1=ALU.add),
                        ["yn", "st1e", L("lnb")], ["R1d"])
                for g in range(8):
                    b = bank()
                    for c in range(NCH):
                        o = ps[b][:, c * 128:(c + 1) * 128]
                        pe(lambda e, o=o, c=c, g=g, s=s: e.matmul(o, lhsT=vn[:, c, g * 128:(g + 1) * 128], rhs=s["WsT"][:, g, :], start=True, stop=False),
                           ["R1d", L("WsT")], [PK(b)])
                        pe(lambda e, o=o, g=g, s=s: e.matmul(o, lhsT=ones_row[:, :], rhs=s["brow"][:, g * 128:(g + 1) * 128], start=False, stop=True),
                           ["ones_row", L("bsrhi"), L("bsrlo")], [PK(b)])
                    dve(lambda e, b=b, g=g: e.tensor_tensor(out=yaT[:, g, :], in0=ps[b][:, 0:T], in1=gu[:, g, :], op=ALU.mult), [PK(b), "R1a"], [("yaT", g)])
                P.mark("t%d_l%d_B" % (it, l))
                dve(lambda e, s=s: e.tensor_copy(out=xbcT[:, :, 0:3], in_=s["halo"][:, :, :]), [L("halo")], ["R2a", "R2b"])
                for q in range(8):
                    proj_fm(l, C_XBC + q * 512, lambda b, jj, q=q: act(
                        lambda e: e.activation(out=xbcT[:, q * 4 + jj, 3:3 + T], in_=ps[b][:, 0:T], func=AF.Copy), [PK(b)], ["R2a" if q < 4 else "R2b"]))
                dve(lambda e, s=s: e.tensor_copy(out=s["halo"][:, :, :], in_=xbcT[:, :, T:T + 3]), ["R2a", "R2b"], [L("halo")])
                P.mark("t%d_l%d_dt" % (it, l))
                bdt = bank()
                for c in range(NCH):
                    for k in range(8):
                        pe(lambda e, c=c, k=k, s=s, bdt=bdt: e.matmul(ps[bdt][:, c * 32:(c + 1) * 32], lhsT=hT[:, k, c * 128:(c + 1) * 128], rhs=s["wdt"][:, k, :],
                                                                       start=(k == 0), stop=(k == 7)),
                           [("hT", c), L("wdt")], [PK(bdt)])
                dve(lambda e, s=s, bdt=bdt: e.tensor_tensor(out=dtb[:, :, :], in0=ps[bdt][:, 0:NCH * 32].rearrange("p (c h) -> p c h", c=NCH),
                                                            in1=s["dtbias"][:, :].unsqueeze(1).broadcast_to([128, NCH, 32]), op=ALU.add),
                    [PK(bdt), L("dtbias")], ["dtb"])
                act(lambda e: e.activation(out=dtb[:, :, :], in_=dtb[:, :, :], func=AF.Exp), ["dtb"], ["dtb"])
                act(lambda e: e.activation(out=dtt[:, :, :], in_=dtb[:, :, :], func=AF.Ln, bias=1.0), ["dtb"], ["dtt"])
                dve(lambda e, s=s: e.tensor_tensor(out=adt[:, :, :], in0=dtt[:, :, :], in1=s["a_bc"][:, :].unsqueeze(1).broadcast_to([128, NCH, 32]), op=ALU.mult),
                    ["dtt", L("a_bc")], ["adt"])
                P.mark("t%d_l%d_conv" % (it, l))
                dsl = [wload_raw(l, lambda q=q, d=d: d["diag_bf"][q]) for q in range(2)]
                for c in range(NCH):
                    for qq in range(4):
                        b = bank()
                        for jj in range(4):
                            j = qq * 4 + jj
                            dg = wsl[dsl[j // 8]][:, :].rearrange("p (j k n) -> p j k n", j=8, k=4)
                            o = ps[b][:, jj * 128:(jj + 1) * 128]
                            for k in range(4):
                                pe(lambda e, o=o, j=j, k=k, c=c, dg=dg: e.matmul(o, lhsT=xbcT[:, j, c * 128 + k:c * 128 + k + 128], rhs=dg[:, j % 8, k, :],
                                                                                start=(k == 0), stop=False),
                                   ["R2a", ("w", dsl[j // 8])], [PK(b)])
                            pe(lambda e, o=o, j=j, s=s: e.matmul(o, lhsT=sel3[:, j // 8, :], rhs=s["crow"][:, (j % 8) * 128:(j % 8 + 1) * 128], start=False, stop=True),
                               ["sel3", L("cbrhi")], [PK(b)])
                        act(lambda e, b=b, c=c, qq=qq: e.activation(out=xs_tok[:, c, qq * 512:(qq + 1) * 512], in_=ps[b][:, :], func=AF.Silu), [PK(b)], [XS(c)])
                dsl2 = [wload_raw(l, lambda q=q, d=d: d["diag_bf"][q]) for q in (2, 3)]
                dgB = wsl[dsl2[0]][:, :].rearrange("p (j k n) -> p j k n", j=8, k=4)
                dgC = wsl[dsl2[1]][:, :].rearrange("p (j k n) -> p j k n", j=8, k=4)
                for g in range(8):
                    for (dg, j, dst, key) in ((dgB, 16 + g, BT, "R1dB"), (dgC, 24 + g, CT, "CT")):
                        b = bank()
                        for k in range(4):
                            pe(lambda e, b=b, j=j, g=g, k=k, dg=dg: e.matmul(ps[b][:, 0:T], lhsT=dg[:, g, k, :], rhs=xbcT[:, j, k:k + T], start=(k == 0), stop=(k == 3)),
                               ["R2b", ("w", dsl2[0]), ("w", dsl2[1])], [PK(b)])
                        wk = ["R1d"] if key == "R1dB" else [(key, g)]
                        act(lambda e, b=b, j=j, g=g, dst=dst, s=s: e.activation(out=dst[:, g, :], in_=ps[b][:, 0:T], func=AF.Silu, bias=s["convb"][:, j:j + 1]),
                            [PK(b), L("convb")], wk)
                for c in range(NCH):
                    b = bank()
                    for g in range(8):
                        pe(lambda e, b=b, g=g, c=c: e.transpose(out=psb[b][:, g * 128:(g + 1) * 128], in_=BT[:, g, c * 128:(c + 1) * 128], identity=ident[:]),
                           ["R1d", "ident"], [PK(b)])
                    act(lambda e, b=b, c=c: e.activation(out=B_tok[:, c, :], in_=psb[b][:, :], func=AF.Copy), [PK(b)], [("R1c", c)])
                P.mark("t%d_l%d_sz" % (it, l))
                for q in range(4):
                    proj_tm(l, C_SZ + q * 512, lambda b, c, q=q: act(
                        lambda e: e.activation(out=szs[:, c, q * 512:(q + 1) * 512], in_=ps[b][:, :], func=AF.Silu), [PK(b)], ["R2a"]))
                P.mark("t%d_l%d_ssd" % (it, l))
                bs_ = bank()
                for c in range(NCH):
                    pe(lambda e, c=c, bs_=bs_: e.matmul(ps[bs_][:, c * 64:c * 64 + 32], lhsT=triu[:, :], rhs=adt[:, c, :], start=True, stop=True), ["triu", "adt"], [PK(bs_)])
                    pe(lambda e, c=c, bs_=bs_: e.matmul(ps[bs_][:, c * 64 + 32:c * 64 + 64], lhsT=onesf[:, :], rhs=adt[:, c, :], start=True, stop=True), ["onesf", "adt"], [PK(bs_)])
                psv = ps[bs_][:, 0:NCH * 64].rearrange("p (c x) -> p c x", c=NCH)
                dve(lambda e, psv=psv: e.tensor_copy(out=acs_a[:, :, :], in_=psv[:, :, 0:32]), [PK(bs_)], ["acs"])
                act(lambda e, psv=psv: e.activation(out=el_a[:, :, :], in_=psv[:, :, 0:32], func=AF.Exp), [PK(bs_)], ["el"])
                act(lambda e, psv=psv: e.activation(out=dec_a[:, :, :], in_=psv[:, :, 32:64], func=AF.Exp), [PK(bs_)], ["dec"])
                dve(lambda e, psv=psv: e.tensor_tensor(out=dd_a[:, :, :], in0=psv[:, :, 32:64], in1=acs_a[:, :, :], op=ALU.subtract), [PK(bs_), "acs"], ["dd"])
                act(lambda e: e.activation(out=dd_a[:, :, :], in_=dd_a[:, :, :], func=AF.Exp), ["dd"], ["dd"])
                dve(lambda e: e.tensor_tensor(out=dtdte_a[:, :, :], in0=dtt[:, :, :], in1=dd_a[:, :, :], op=ALU.mult), ["dtt", "dd"], ["dtdte"])
                sctr = [0, 0]

                def bankD():
                    sctr[0] += 1
                    return sctr[0] % 2

                def bankS():
                    sctr[1] += 1
                    return 2 + sctr[1] % 6

                def emit_rhsD(c, g):
                    i3 = g % 2
                    pool(lambda e: e.tensor_tensor(out=rhsD[i3][:, :, :], in0=adt[:, c, 4 * g:4 * g + 4].unsqueeze(2).broadcast_to([128, 4, 128]),
                                                   in1=triu[:, :].unsqueeze(1).broadcast_to([128, 4, 128]), op=ALU.mult),
                         ["adt", "triu"], [("rhsD", i3)])
                    bD = bankD()
                    pe(lambda e: e.matmul(ps[bD][:, :], lhsT=mgt[:, :], rhs=rhsD[i3][:, :, :].rearrange("p r l -> p (r l)"), start=True, stop=True),
                       ["mgt", ("rhsD", i3)], [PK(bD)])
                    return bD

                def emit_G(c, g, bD):
                    i2 = g % 2
                    i4 = g % 4
                    act(lambda e: e.activation(out=expD[i2][:, :, :].rearrange("p r l -> p (r l)"), in_=ps[bD][:, :], func=AF.Exp), [PK(bD)], [("expD", i2)])
                    dve(lambda e: e.tensor_tensor(out=Gt[i4][:, :, :], in0=expD[i2][:, :, :], in1=CBm[:, g, :].unsqueeze(1).broadcast_to([128, 4, 128]), op=ALU.mult),
                        [("expD", i2), ("CBm", g // 4)], [("G", i4)])

                def head(c):
                    for half in range(2):
                        b = bankS()
                        for gg in range(4):
                            g = half * 4 + gg
                            pe(lambda e, b=b, gg=gg, g=g: e.matmul(ps[b][:, gg * 128:(gg + 1) * 128], lhsT=BT[:, g, c * 128:(c + 1) * 128], rhs=CT[:, g, c * 128:(c + 1) * 128],
                                                                   start=True, stop=True),
                               ["R1d", ("CT", g)], [PK(b)])
                        dve(lambda e, b=b, half=half: e.tensor_tensor(out=CBm[:, half * 4:(half + 1) * 4, :], in0=ps[b][:, :].rearrange("p (g l) -> p g l", g=4),
                                                                      in1=triu[:, :].unsqueeze(1).broadcast_to([128, 4, 128]), op=ALU.mult),
                            [PK(b), "triu"], [("CBm", half)])
                    return {0: emit_rhsD(c, 0), 1: emit_rhsD(c, 1)}

                pending_tail = []
                bDs_next = head(0)
                for c in range(NCH):
                    bDs = bDs_next
                    xsv = xs_tok[:, c, :].rearrange("p (h q) -> p h q", h=32)
                    pool(lambda e, c=c, xsv=xsv: e.tensor_tensor(out=xd[:, :, :], in0=xsv, in1=dtt[:, c, :].unsqueeze(2).broadcast_to([128, 32, 64]), op=ALU.mult),
                         [XS(c), "dtt"], ["xd"])
                    pool(lambda e, xsv=xsv, s=s: e.tensor_tensor(out=xsD[:, :, :], in0=xsv, in1=s["dskip"][:, :].unsqueeze(2).broadcast_to([128, 32, 64]), op=ALU.mult),
                         [XS(c), L("dskip")], ["xsD"])
                    pool(lambda e, c=c, xsv=xsv: e.tensor_tensor(out=xdte[:, :, :], in0=xsv, in1=dtdte_a[:, c, :].unsqueeze(2).broadcast_to([128, 32, 64]), op=ALU.mult),
                         [XS(c), "dtdte"], ["xdte"])
                    for f in pending_tail:
                        f()
                    pending_tail = []
                    emit_G(c, 0, bDs.pop(0))
                    emit_G(c, 1, bDs.pop(1))
                    for gp in range(4):
                        byd, byo, bst = bankS(), bankS(), bankS()
                        if gp + 1 < 4:
                            for g_ in (2 * gp + 2, 2 * gp + 3):
                                bDs[g_] = emit_rhsD(c, g_)
                        for gi in range(2):
                            g = gp * 2 + gi
                            i4 = g % 4
                            for r in range(4):
                                h = 4 * g + r
                                o = ps[byd][:, gi * 256 + r * 64:gi * 256 + (r + 1) * 64]
                                pe(lambda e, o=o, i4=i4, r=r, h=h: e.matmul(o, lhsT=Gt[i4][:, r, :], rhs=xd[:, h, :], start=True, stop=False), [("G", i4), "xd"], [PK(byd)])
                                pe(lambda e, o=o, h=h: e.matmul(o, lhsT=ident[:, :], rhs=xsD[:, h, :], start=False, stop=True), ["ident", "xsD"], [PK(byd)])
                            pe(lambda e, g=g, gi=gi, c=c, s=s, byo=byo: e.matmul(ps[byo][:, gi * 256:(gi + 1) * 256], lhsT=CT[:, g, c * 128:(c + 1) * 128], rhs=s["Sb"][:, g, :],
                                                                                 start=True, stop=True),
                               [("CT", g), L(("Sb", gp))], [PK(byo)])
                            pe(lambda e, g=g, gi=gi, c=c, bst=bst: e.matmul(ps[bst][:, gi * 256:(gi + 1) * 256], lhsT=B_tok[:, c, g * 128:(g + 1) * 128],
                                                                           rhs=xdte[:, 4 * g:4 * g + 4, :].rearrange("p r q -> p (r q)"), start=True, stop=True),
                               [("R1c", c), "xdte"], [PK(bst)])
                        if gp + 1 < 4:
                            for g_ in (2 * gp + 2, 2 * gp + 3):
                                emit_G(c, g_, bDs.pop(g_))
                        tb = tmpy[gp % 2]
                        h0 = gp * 8
                        dve(lambda e, tb=tb, h0=h0, byo=byo, c=c: e.tensor_tensor(out=tb[:, :, :], in0=ps[byo][:, :].rearrange("p (h q) -> p h q", h=8),
                                                                                   in1=el_a[:, c, h0:h0 + 8].unsqueeze(2).broadcast_to([128, 8, 64]), op=ALU.mult),
                            [PK(byo), "el"], [("tmpy", 0)])
                        dve(lambda e, tb=tb, byd=byd: e.tensor_tensor(out=tb[:, :, :], in0=ps[byd][:, :].rearrange("p (h q) -> p h q", h=8), in1=tb[:, :, :], op=ALU.add),
                            [PK(byd), ("tmpy", 0)], [("tmpy", 0)])
                        dve(lambda e, tb=tb, gp=gp, c=c: e.tensor_tensor(out=yz[:, 2 * gp:2 * gp + 2, :].rearrange("p g n -> p (g n)"), in0=tb[:, :, :].rearrange("p h q -> p (h q)"),
                                                                        in1=szs[:, c, gp * 512:(gp + 1) * 512], op=ALU.mult),
                            [("tmpy", 0), "R2a"], [("yz", gp)])
                        Sv = s["S"][:, 2 * gp:2 * gp + 2, :].rearrange("p g (r q) -> p (g r) q", r=4)
                        dve(lambda e, Sv=Sv, h0=h0, c=c: e.tensor_tensor(out=Sv, in0=Sv, in1=dec_a[:, c, h0:h0 + 8].unsqueeze(2).broadcast_to([128, 8, 64]), op=ALU.mult),
                            [L(("S", gp)), "dec"], [L(("S", gp))])
                        dve(lambda e, gp=gp, s=s, bst=bst: e.tensor_tensor(out=s["S"][:, 2 * gp:2 * gp + 2, :].rearrange("p g n -> p (g n)"), in0=ps[bst][:, :],
                                                                          in1=s["S"][:, 2 * gp:2 * gp + 2, :].rearrange("p g n -> p (g n)"), op=ALU.add),
                            [PK(bst), L(("S", gp))], [L(("S", gp))])
                        act(lambda e, gp=gp, s=s: e.activation(out=s["Sb"][:, 2 * gp:2 * gp + 2, :], in_=s["S"][:, 2 * gp:2 * gp + 2, :], func=AF.Copy), [L(("S", gp))], [L(("Sb", gp))])
                    if c + 1 < NCH:
                        bDs_next = head(c + 1)
                    for g in range(8):
                        act(lambda e, g=g: e.activation(out=xn[:, 0:256], in_=yz[:, g, :], func=AF.Square, scale=1.0 / 16, accum_out=ssq[:, g:g + 1]), [("yz", g // 2)], ["xn", ("ssq", g)])
                    dve(lambda e: e.tensor_scalar(out=ssq[:, :], in0=ssq[:, :], scalar1=EPS, scalar2=None, op0=ALU.add), [("ssq", g) for g in range(8)], [("ssq", g) for g in range(8)])
                    pool(lambda e: e.tensor_tensor(out=rstd8[:, :], in0=ssq[:, :], in1=neghalf[:, :], op=ALU.pow), [("ssq", g) for g in range(8)] + ["neghalf"], ["rstd8"])
                    for g in range(8):
                        act(lambda e, g=g: e.activation(out=yn[:, g, :], in_=yz[:, g, :], func=AF.Copy, scale=rstd8[:, g:g + 1]), ["rstd8", ("yz", g // 2)], ["yn"])

                    def tail(c=c):
                        ynf = yn[:, :, :].rearrange("p g n -> p (g n)")
                        for half in range(2):
                            b = bankS()
                            for kk in range(8):
                                k = half * 8 + kk
                                pe(lambda e, b=b, kk=kk, k=k: e.transpose(out=psb[b][:, kk * 128:(kk + 1) * 128], in_=ynf[:, k * 128:(k + 1) * 128], identity=ident[:]),
                                   ["yn", "ident"], [PK(b)])
                            act(lambda e, b=b, half=half: e.activation(out=ynT[:, half * 8:(half + 1) * 8, c * 128:(c + 1) * 128],
                                                                       in_=psb[b][:, :].rearrange("p (k t) -> p k t", k=8), func=AF.Copy),
                                [PK(b)], ["R2b"])
                    pending_tail = [tail]
                for f in pending_tail:
                    f()
                P.mark("t%d_l%d_C" % (it, l))
                for h2 in range(2):
                    proj_fm(l, C_GA + h2 * 512, lambda b, jj, h2=h2: act(
                        lambda e: e.activation(out=sga[:, h2 * 4 + jj, :], in_=ps[b][:, 0:T], func=AF.Sigmoid), [PK(b)], ["R1a"]))
                for h2 in range(2):
                    proj_fm(l, C_GB + h2 * 512, lambda b, jj, h2=h2: act(
                        lambda e: e.activation(out=sgb[:, h2 * 4 + jj, :], in_=ps[b][:, 0:T], func=AF.Sigmoid), [PK(b)], ["R1b"]))
                for h2 in range(2):
                    sl = wload(l, blk(l, "wa_bf", h2 * 512))
                    wv = wview(sl)
                    for jj in range(4):
                        j = h2 * 4 + jj
                        b = bank()
                        for k in range(8):
                            pe(lambda e, b=b, k=k, jj=jj, wv=wv: e.matmul(ps[b][:, 0:T], lhsT=wv[:, k, jj * 128:(jj + 1) * 128], rhs=yaT[:, k, :], start=(k == 0), stop=(k == 7)),
                               [("w", sl)] + [("yaT", g) for g in range(8)], [PK(b)])
                        dve(lambda e, b=b, j=j: e.tensor_tensor(out=sga[:, j, :], in0=ps[b][:, 0:T], in1=sga[:, j, :], op=ALU.mult), [PK(b), "R1a"], ["R1a"])
                for h2 in range(2):
                    sls = [wload(l, blk(l, "wb_bf", h2 * 512, r0=8 * kh)) for kh in range(2)]
                    for jj in range(4):
                        j = h2 * 4 + jj
                        b = bank()
                        for k in range(16):
                            wv = wview(sls[k // 8])
                            pe(lambda e, b=b, k=k, jj=jj, wv=wv: e.matmul(ps[b][:, 0:T], lhsT=wv[:, k % 8, jj * 128:(jj + 1) * 128], rhs=ynT[:, k, :], start=(k == 0), stop=(k == 15)),
                               [("w", sls[k // 8]), "R2b"], [PK(b)])
                        dve(lambda e, b=b, j=j: e.tensor_tensor(out=sgb[:, j, :], in0=ps[b][:, 0:T], in1=sgb[:, j, :], op=ALU.mult), [PK(b), "R1b"], ["R1b"])
                        dve(lambda e, j=j: e.tensor_tensor(out=mT[:, j, :], in0=sgb[:, j, :], in1=sga[:, j, :], op=ALU.add), ["R1a", "R1b"], HT_ALL + [("mT", j)])
                slo = [wload(l, blk(l, "wo_bf", h2 * 512)) for h2 in range(2)]
                for c in range(NCH):
                    for h2 in range(2):
                        wv = wview(slo[h2])
                        b = bank()
                        for k in range(8):
                            pe(lambda e, b=b, k=k, c=c, wv=wv: e.matmul(ps[b][:, :], lhsT=mT[:, k, c * 128:(c + 1) * 128], rhs=wv[:, k, :], start=(k == 0), stop=(k == 7)),
                               [("w", slo[h2])] + [("mT", j) for j in range(8)] + HT_ALL, [PK(b)])
                        tb = tmpy[h2]
                        tbf = tb[:, :, :].rearrange("p h q -> p (h q)")
                        dve(lambda e, b=b, tbf=tbf, h2=h2, s=s: e.tensor_tensor(out=tbf, in0=ps[b][:, :], in1=s["gate"][:, h2 * 512:(h2 + 1) * 512], op=ALU.mult),
                            [PK(b), L("gate")], [("tmpy", 0)])
                        dve(lambda e, tbf=tbf, c=c, h2=h2: e.tensor_tensor(out=xt[:, c, h2 * 512:(h2 + 1) * 512], in0=xt[:, c, h2 * 512:(h2 + 1) * 512], in1=tbf, op=ALU.add),
                            [("tmpy", 0), ("xt", c)], [("xt", c)])
                    if l == layers[-1]:
                        emit_out_chunk(si, it, c, dst_out)
        if dump:
            dbgS_d = nc.dram_tensor("dbg", [128, 2048], F32, kind="ExternalOutput").ap()
            P.dma("act", lambda e: e.dma_start(out=dbgS_d, in_=LS[0]["S"][:, :, :].rearrange("p g n -> p (g n)")),
                  [(("S", gp), 0) for gp in range(4)], [], semkey="dbgout")
        if debug is not None:
            if isinstance(debug, str):
                n = dict(P.marks)[debug]
            else:
                n = debug
            print("MARKS", P.marks, "total", len(P.order), "truncate at", n)
            P.truncate(n)
            P.dma("act", lambda e: e.dma_start(out=out_d[S - T:S, :].rearrange("(c p) n -> p c n", p=128), in_=xt[:, :, :]),
                  [("xt", c) for c in range(NCH)], [], semkey="xout_dbg")
        P.emit(nc)
    return nc


def _layer_maps(inp, l):
    f = lambda a: np.ascontiguousarray(a, dtype=np.float32)
    col = lambda v: f(np.asarray(v).reshape(-1, 128).T)
    bc = lambda v: f(np.broadcast_to(np.asarray(v).reshape(1, -1), (128, np.asarray(v).size)))
    m = {}
    m["ada_w_%d" % l] = f(inp["ada_w"][l])
    m["adab_col_%d" % l] = col(inp["ada_b"][l])
    m["adab_gate_bc_%d" % l] = bc(inp["ada_b"][l][2048:3072])
    m["normw_col_%d" % l] = col(inp["norm_w"][l])
    m["w_in_%d" % l] = f(inp["w_in"][l])
    m["lnw_bc_%d" % l] = bc(inp["gm_ln_w"][l])
    m["lnb_bc_%d" % l] = bc(inp["gm_ln_b"][l])
    m["gm_ws_%d" % l] = f(inp["gm_ws"][l])
    m["bs_row_%d" % l] = f(np.asarray(inp["gm_bs"][l]).reshape(1, 1024))
    m["convw_col_%d" % l] = f(np.asarray(inp["conv_w"][l]).reshape(4, 32, 128).transpose(2, 0, 1))
    m["convb_col_%d" % l] = col(inp["conv_b"][l])
    m["convb_row_%d" % l] = f(np.asarray(inp["conv_b"][l]).reshape(1, 4096))
    m["dtbias_bc_%d" % l] = bc(inp["dt_bias"][l])
    m["alog_bc_%d" % l] = bc(inp["a_log"][l])
    m["dskip_bc_%d" % l] = bc(inp["d_skip"][l])
    m["ssmnw_col_%d" % l] = col(inp["ssm_norm_w"][l])
    m["w_proj_a_%d" % l] = f(inp["w_proj_a"][l])
    m["w_proj_b_%d" % l] = f(inp["w_proj_b"][l])
    m["w_out_%d" % l] = f(inp["w_out"][l])
    return m


_NC_CACHE = {}


def _get_nc(S, NL, NCH, do_final):
    key = (S, NL, NCH, do_final)
    if key not in _NC_CACHE:
        _NC_CACHE[key] = build(S, NL, NCH, do_final)
    return _NC_CACHE[key]


def run_layers(x, inp, layers, do_final, NCH=4, n_cores=8):
    B, S, _ = x.shape
    nc = _get_nc(S, len(layers), NCH, do_final)
    in_maps = []
    for core in range(n_cores):
        b = core % B
        m = {"x": np.ascontiguousarray(x[b], dtype=np.float32),
             "c_col": np.ascontiguousarray(np.asarray(inp["c"][b]).reshape(8, 128).T, dtype=np.float32),
             "fnw_bc": np.ascontiguousarray(np.broadcast_to(np.asarray(inp["final_norm_w"]).reshape(1, 1024), (128, 1024)), dtype=np.float32)}
        for i, l in enumerate(layers):
            lm = _layer_maps(inp, l)
            for k, v in lm.items():
                base = k.rsplit("_", 1)[0]
                m["%s_%d" % (base, i)] = v
        in_maps.append(m)
    res = run_bass_kernel_spmd(nc, in_maps, core_ids=list(range(n_cores)))
    return np.stack([res.results[b]["out"] for b in range(B)], axis=0)


def kernel(**inputs):
    inp = {k: np.asarray(v) for k, v in inputs.items()}
    x = inp["x"]
    out = run_layers(x, inp, [0, 1], True)
    return out.astype(np.float32)
```
